# Optimizing a Trainium2 kernel written in Bass

```python
import jax, jax.numpy as jnp
from jax import lax
import numpy as np

D_MODEL = 4096
BATCH = 2
SEQ = 8192
DEPTH = 2

CHUNK = 64
Q_BLOCK = 128
N_A_LAYERS = DEPTH // 2
N_B_LAYERS = DEPTH - N_A_LAYERS
HEAD_DIM = 128
FOX_HEADS = D_MODEL // HEAD_DIM
FOX_HEAD_DIM = HEAD_DIM
MLA_HEADS = D_MODEL // HEAD_DIM
MLA_NOPE_DIM = 128
MLA_ROPE_DIM = 64
MLA_V_DIM = 128
MLA_Q_LORA = 1024
MLA_KV_LORA = 512
ROPE_THETA = 10000.0
D_FF = 11008
CONV_WIDTH = 3
DEEPNORM_ALPHA = (2.0 * DEPTH) ** 0.25
DEEPNORM_BETA = (8.0 * DEPTH) ** -0.25
LN_EPS = 1e-5
RMS_EPS = 1e-6
FORGET_BIAS_MEAN = 3.0
ADA_INIT_GAIN = 0.1

kernel_name = "fox_mla_yoco_convffn_deepnorm_adaln"


def _layer_norm(x, g, b):
    xf = x.astype(jnp.float32)
    mu = jnp.mean(xf, axis=-1, keepdims=True)
    var = jnp.mean(jnp.square(xf - mu), axis=-1, keepdims=True)
    return ((xf - mu) * lax.rsqrt(var + LN_EPS) * g + b).astype(x.dtype)


def _rms_norm(x, g):
    xf = x.astype(jnp.float32)
    return (xf * lax.rsqrt(jnp.mean(xf * xf, axis=-1, keepdims=True) + RMS_EPS) * g).astype(x.dtype)


def _rope_tables(seq):
    inv = ROPE_THETA ** (-jnp.arange(0, MLA_ROPE_DIM, 2, dtype=jnp.float32) / MLA_ROPE_DIM)
    ang = jnp.arange(seq, dtype=jnp.float32)[:, None] * inv[None, :]
    return jnp.cos(ang), jnp.sin(ang)


def _apply_rope(x, cos, sin):
    x1, x2 = jnp.split(x.astype(jnp.float32), 2, axis=-1)
    shape = (1, cos.shape[0]) + (1,) * (x.ndim - 3) + (cos.shape[1],)
    c, s = cos.reshape(shape), sin.reshape(shape)
    return jnp.concatenate([x1 * c - x2 * s, x1 * s + x2 * c], axis=-1).astype(x.dtype)


def _sweep_query_blocks(score_block, v):
    b, s, h, dv = v.shape

    def one_block(i):
        p = jax.nn.softmax(score_block(i * Q_BLOCK), axis=-1).astype(v.dtype)
        return jnp.einsum("bhqk,bkhd->bqhd", p, v)

    out = lax.map(one_block, jnp.arange(s // Q_BLOCK))
    return jnp.moveaxis(out, 0, 1).reshape(b, s, h * dv)


def _forgetting_attention(h, w_qkv, w_f, b_f, w_o):
    b, s, _ = h.shape
    qkv = (h @ w_qkv).reshape(b, s, 3, FOX_HEADS, FOX_HEAD_DIM)
    q, k, v = qkv[:, :, 0], qkv[:, :, 1], qkv[:, :, 2]
    log_f = jax.nn.log_sigmoid((h @ w_f + b_f).astype(jnp.float32))
    cum = jnp.cumsum(log_f, axis=1).transpose(0, 2, 1)
    scale = FOX_HEAD_DIM ** -0.5
    key_pos = jnp.arange(s)

    def score_block(start):
        qb = lax.dynamic_slice_in_dim(q, start, Q_BLOCK, axis=1)
        cq = lax.dynamic_slice_in_dim(cum, start, Q_BLOCK, axis=2)
        logits = jnp.einsum("bqhd,bkhd->bhqk", qb, k, preferred_element_type=jnp.float32) * scale
        logits = logits + (cq[..., :, None] - cum[..., None, :])
        mask = key_pos[None, :] <= (start + jnp.arange(Q_BLOCK))[:, None]
        return jnp.where(mask, logits, -jnp.inf)

    return _sweep_query_blocks(score_block, v) @ w_o


def _mla_shared_kv(hs, w_dkv, kv_norm, w_ukv, cos, sin):
    b, s, _ = hs.shape
    ckv_kr = hs @ w_dkv
    c_kv = _rms_norm(ckv_kr[..., :MLA_KV_LORA], kv_norm)
    k_rope = _apply_rope(ckv_kr[..., MLA_KV_LORA:], cos, sin)
    kv = (c_kv @ w_ukv).reshape(b, s, MLA_HEADS, MLA_NOPE_DIM + MLA_V_DIM)
    return kv[..., :MLA_NOPE_DIM], k_rope, kv[..., MLA_NOPE_DIM:]


def _mla_attention(h, k_nope, k_rope, v, w_dq, q_norm, w_uq, w_o, cos, sin):
    b, s, _ = h.shape
    q = (_rms_norm(h @ w_dq, q_norm) @ w_uq).reshape(b, s, MLA_HEADS, MLA_NOPE_DIM + MLA_ROPE_DIM)
    q_nope = q[..., :MLA_NOPE_DIM]
    q_rope = _apply_rope(q[..., MLA_NOPE_DIM:], cos, sin)
    scale = (MLA_NOPE_DIM + MLA_ROPE_DIM) ** -0.5
    key_chunk = jnp.arange(s) // CHUNK

    def score_block(start):
        qn = lax.dynamic_slice_in_dim(q_nope, start, Q_BLOCK, axis=1)
        qr = lax.dynamic_slice_in_dim(q_rope, start, Q_BLOCK, axis=1)
        logits = (jnp.einsum("bqhd,bkhd->bhqk", qn, k_nope, preferred_element_type=jnp.float32)
                  + jnp.einsum("bqhr,bkr->bhqk", qr, k_rope, preferred_element_type=jnp.float32)) * scale
        q_chunk = (start + jnp.arange(Q_BLOCK)) // CHUNK
        mask = key_chunk[None, :] <= q_chunk[:, None]
        return jnp.where(mask, logits, -jnp.inf)

    return _sweep_query_blocks(score_block, v) @ w_o


def _conv_ffn(h, w_up, conv_w, conv_b, w_down):
    s = h.shape[1]
    u = h @ w_up
    u_pad = jnp.pad(u, ((0, 0), (CONV_WIDTH - 1, 0), (0, 0)))
    u = sum((conv_w[j] * u_pad[:, j:j + s] for j in range(CONV_WIDTH)), conv_b)
    a, g = jnp.split(u, 2, axis=-1)
    return (jax.nn.silu(g) * a) @ w_down


def _modulate(ada, x):
    shift, scale, gate = jnp.split(ada, 3, axis=-1)
    return x * (1.0 + scale[:, None]) + shift[:, None], 1.0 + gate[:, None]


def setup_inputs(seed: int = 0) -> dict:
    key = jax.random.key(seed)
    ks = jax.random.split(key, 22)
    D = D_MODEL

    def nrm(k, shape, fan_in, gain=1.0):
        return jax.random.normal(k, shape, jnp.float32) * (gain * fan_in ** -0.5)

    def small(k, shape, s=0.02):
        return s * jax.random.normal(k, shape, jnp.float32)

    return {
        "x": jax.random.normal(ks[0], (BATCH, SEQ, D), jnp.float32),
        "c": jax.random.normal(ks[1], (BATCH, D), jnp.float32),
        "ada_w": nrm(ks[2], (DEPTH, 2, D, 3 * D), D, ADA_INIT_GAIN),
        "ada_b": small(ks[3], (DEPTH, 2, 3 * D)),
        "ln_g": 1.0 + small(ks[4], (DEPTH, 2, D)),
        "ln_b": small(ks[5], (DEPTH, 2, D)),
        "fox_w_qkv": nrm(ks[6], (N_A_LAYERS, D, 3 * FOX_HEADS * FOX_HEAD_DIM), D),
        "fox_w_f": nrm(ks[7], (N_A_LAYERS, D, FOX_HEADS), D),
        "fox_b_f": FORGET_BIAS_MEAN + small(ks[8], (N_A_LAYERS, FOX_HEADS), 0.5),
        "fox_w_o": nrm(ks[9], (N_A_LAYERS, FOX_HEADS * FOX_HEAD_DIM, D), FOX_HEADS * FOX_HEAD_DIM, DEEPNORM_BETA),
        "mla_w_dq": nrm(ks[10], (N_B_LAYERS, D, MLA_Q_LORA), D),
        "mla_q_norm": 1.0 + small(ks[11], (N_B_LAYERS, MLA_Q_LORA)),
        "mla_w_uq": nrm(ks[12], (N_B_LAYERS, MLA_Q_LORA, MLA_HEADS * (MLA_NOPE_DIM + MLA_ROPE_DIM)), MLA_Q_LORA),
        "mla_w_o": nrm(ks[13], (N_B_LAYERS, MLA_HEADS * MLA_V_DIM, D), MLA_HEADS * MLA_V_DIM, DEEPNORM_BETA),
        "mla_w_dkv": nrm(ks[14], (D, MLA_KV_LORA + MLA_ROPE_DIM), D),
        "mla_kv_norm": 1.0 + small(ks[15], (MLA_KV_LORA,)),
        "mla_w_ukv": nrm(ks[16], (MLA_KV_LORA, MLA_HEADS * (MLA_NOPE_DIM + MLA_V_DIM)), MLA_KV_LORA),
        "ffn_w_up": nrm(ks[17], (DEPTH, D, 2 * D_FF), D),
        "ffn_conv_w": nrm(ks[18], (DEPTH, CONV_WIDTH, 2 * D_FF), CONV_WIDTH),
        "ffn_conv_b": small(ks[19], (DEPTH, 2 * D_FF)),
        "ffn_w_down": nrm(ks[20], (DEPTH, D_FF, D), D_FF, DEEPNORM_BETA),
    }


def reference(x, c, ada_w, ada_b, ln_g, ln_b, fox_w_qkv, fox_w_f, fox_b_f, fox_w_o,
              mla_w_dq, mla_q_norm, mla_w_uq, mla_w_o, mla_w_dkv, mla_kv_norm, mla_w_ukv,
              ffn_w_up, ffn_conv_w, ffn_conv_b, ffn_w_down):
    cos, sin = _rope_tables(x.shape[1])
    c_act = jax.nn.silu(c)
    k_nope = k_rope = v_shared = None
    for layer in range(DEPTH):
        ada = jnp.einsum("bd,mde->mbe", c_act, ada_w[layer]) + ada_b[layer][:, None]
        h, gate = _modulate(ada[0], x)
        if layer < N_A_LAYERS:
            a = layer
            mix = _forgetting_attention(h, fox_w_qkv[a], fox_w_f[a], fox_b_f[a], fox_w_o[a])
        else:
            if layer == N_A_LAYERS:
                k_nope, k_rope, v_shared = _mla_shared_kv(x, mla_w_dkv, mla_kv_norm, mla_w_ukv, cos, sin)
            j = layer - N_A_LAYERS
            mix = _mla_attention(h, k_nope, k_rope, v_shared, mla_w_dq[j], mla_q_norm[j],
                                 mla_w_uq[j], mla_w_o[j], cos, sin)
        x = _layer_norm(DEEPNORM_ALPHA * x + gate * mix, ln_g[layer, 0], ln_b[layer, 0])
        h, gate = _modulate(ada[1], x)
        ffn = _conv_ffn(h, ffn_w_up[layer], ffn_conv_w[layer], ffn_conv_b[layer], ffn_w_down[layer])
        x = _layer_norm(DEEPNORM_ALPHA * x + gate * ffn, ln_g[layer, 1], ln_b[layer, 1])
    return x
```

```python
from contextlib import ExitStack
import numpy as np
import concourse.bass as bass
import concourse.mybir as mybir
from concourse.bass_utils import run_bass_kernel_spmd

F32 = mybir.dt.float32
BF16 = mybir.dt.bfloat16
AF = mybir.ActivationFunctionType
ALU = mybir.AluOpType

D = 4096
B = 2
SEQ = 8192
NCORE = 8
TPC = 2048
TT = 512
KC = D // 128
DFF = 11008
FC = DFF // 128
FQ = 22
QB = [0, 22, 44, 65, 86]
ALPHA = (2.0 * 2) ** 0.25
LN_EPS = 1e-5 / (ALPHA * ALPHA)
RMS_EPS = 1e-6
NEG = -30000.0


class _Sem:
    def __init__(self, handle, step):
        self.h = handle
        self.step = step
        self.count = 0


class _Res:
    __slots__ = ("w", "r")

    def __init__(self):
        self.w = None
        self.r = {}


class Sched:
    def __init__(self, nc, stack):
        self.nc = nc
        self.stack = stack
        self.res = {}
        self.engs = {}
        self.nsem = 0
        for name, e in (("pe", nc.tensor), ("act", nc.scalar), ("dve", nc.vector),
                        ("pool", nc.gpsimd), ("sp", nc.sync)):
            sem = _Sem(self._newsem("e_" + name), 1)
            self.engs[name] = dict(e=e, sem=sem, seen={}, name=name)

    def _newsem(self, name):
        self.nsem += 1
        return self.stack.enter_context(self.nc.semaphore(f"{name}_{self.nsem}"))

    def dma_sem(self, name="d"):
        return _Sem(self._newsem(name), 16)

    def _r(self, key):
        r = self.res.get(key)
        if r is None:
            r = self.res[key] = _Res()
        return r

    def _wait_deps(self, E, reads, writes):
        deps = {}
        for k in reads:
            r = self.res.get(k)
            if r is not None and r.w is not None:
                s, v = r.w
                if deps.get(s, 0) < v:
                    deps[s] = v
        for k in writes:
            r = self.res.get(k)
            if r is not None:
                if r.w is not None:
                    s, v = r.w
                    if deps.get(s, 0) < v:
                        deps[s] = v
                for s, v in r.r.items():
                    if deps.get(s, 0) < v:
                        deps[s] = v
        seen = E["seen"]
        for s, v in deps.items():
            if s is E["sem"] and E["name"] == "pe":
                continue
            if s.step == 16:
                v = s.count
            if seen.get(s, 0) >= v:
                continue
            E["e"].wait_ge(s.h, v)
            seen[s] = v

    def _register(self, ev, reads, writes):
        for k in reads:
            rr = self._r(k).r
            if rr.get(ev[0], 0) < ev[1]:
                rr[ev[0]] = ev[1]
        for k in writes:
            r = self._r(k)
            r.w = ev
            r.r = {}

    def op(self, eng, fn, reads=(), writes=(), mark=True):
        E = self.engs[eng]
        self._wait_deps(E, reads, writes)
        ins = fn(E["e"])
        s = E["sem"]
        if mark:
            ins.then_inc(s.h, 1)
            s.count += 1
            ev = (s, s.count)
        else:
            ev = (s, s.count + 1)
        self._register(ev, reads, writes)
        return ins

    def dma(self, queue, out, in_, reads=(), writes=(), sem=None, **kw):
        E = self.engs[queue]
        self._wait_deps(E, reads, writes)
        ins = E["e"].dma_start(out=out, in_=in_, **kw)
        ins.then_inc(sem.h, 16)
        sem.count += 16
        ev = (sem, sem.count)
        self._register(ev, reads, writes)
        return ev

    def finish(self, queue="sp"):
        E = self.engs[queue]
        allk = list(self.res.keys())
        self._wait_deps(E, allk, allk)


class Ctx:
    def __init__(self):
        self.nc = bass.Bass("TRN2", target_bir_lowering=False)
        self.stack = ExitStack()
        self.S = Sched(self.nc, self.stack)
        self.n = 0

    def sb(self, shape, dt, name="t"):
        self.n += 1
        return self.stack.enter_context(self.nc.sbuf_tensor(f"{name}_{self.n}", list(shape), dt))

    def ps(self, name="ps", shape=(128, 512)):
        self.n += 1
        return self.stack.enter_context(self.nc.psum_tensor(f"{name}_{self.n}", list(shape), F32))

    def din(self, name, shape, dt=F32):
        return self.nc.dram_tensor(name, list(shape), dt, kind="ExternalInput").ap()

    def dout(self, name, shape, dt=F32):
        return self.nc.dram_tensor(name, list(shape), dt, kind="ExternalOutput").ap()

    def dscratch(self, name, shape, dt=F32):
        return self.nc.dram_tensor(name, list(shape), dt, kind="Internal").ap()

    def close(self):
        self.S.finish("sp")
        self.stack.close()
        return self.nc


class Rot:
    def __init__(self, name, bufs, sems=None):
        self.name = name
        self.bufs = bufs
        self.sems = sems
        self.i = -1

    def next(self):
        self.i = (self.i + 1) % len(self.bufs)
        return self.bufs[self.i], (self.name, self.i), (self.sems[self.i] if self.sems else None)


def emit_proj(C, wsrc, wrot, nk, M, rhs_fn, rhs_keys, pst, pkey, W, pcols=None):
    S = C.S
    wb, wkey, wsem = wrot.next()
    S.dma("pool", wb[:, 0:nk, 0:M], wsrc[:, 0:nk, 0:M], writes=[wkey], sem=wsem)
    for k in range(nk):
        S.op("pe", lambda e: e.matmul(pst[0:M, 0:W], lhsT=wb[:, k, 0:M], rhs=rhs_fn(k),
                                      start=(k == 0), stop=(k == nk - 1)),
             reads=[wkey] + rhs_keys(k), writes=[pkey], mark=(k == nk - 1))
    return wb, wkey


def emit_colsum(C, ones, src_fn, src_keys, nchunk, W, psum_t, pkey, sq, tmp_rot):
    S = C.S
    for c in range(nchunk):
        tb, tkey, _ = tmp_rot.next()
        S.op("act", lambda e: e.activation(out=tb[:, 0:W], in_=src_fn(c), func=(AF.Square if sq else AF.Copy)),
             reads=src_keys(c), writes=[tkey])
        S.op("pe", lambda e: e.matmul(psum_t[:, 0:W], lhsT=ones[:, :], rhs=tb[:, 0:W],
                                      start=(c == 0), stop=(c == nchunk - 1)),
             reads=[tkey, "ones"], writes=[pkey])


def emit_ln(C, K, xt, g, cs, W, gcol, bcol, sccol, shcol, hb, hbkey):
    S = C.S
    vec = K["vec"]
    emit_colsum(C, K["ones"], lambda c: xt[:, c, cs], lambda c: [("xt", c, g)], KC, W, K["psL"][0], "psL0", False, K["tmpb"])
    emit_colsum(C, K["ones"], lambda c: xt[:, c, cs], lambda c: [("xt", c, g)], KC, W, K["psL"][1], "psL1", True, K["tmpb"])
    mean, msq, rstd = K["st"][0], K["st"][1], K["st"][2]
    S.op("dve", lambda e: e.tensor_scalar(out=mean[:, 0:W], in0=K["psL"][0][:, 0:W], scalar1=1.0 / D, scalar2=None, op0=ALU.mult),
         reads=["psL0"], writes=["mean"])
    S.op("dve", lambda e: e.tensor_tensor(out=msq[:, 0:W], in0=mean[:, 0:W], in1=mean[:, 0:W], op=ALU.mult),
         reads=["mean"], writes=["msq"])
    S.op("dve", lambda e: e.scalar_tensor_tensor(out=msq[:, 0:W], in0=K["psL"][1][:, 0:W], scalar=1.0 / D, in1=msq[:, 0:W],
                                                 op0=ALU.mult, op1=ALU.subtract),
         reads=["psL1", "msq"], writes=["msq"])
    S.op("dve", lambda e: e.tensor_scalar(out=msq[:, 0:W], in0=msq[:, 0:W], scalar1=LN_EPS, scalar2=None, op0=ALU.add),
         reads=["msq"], writes=["msq"])
    S.op("act", lambda e: e.activation(out=msq[:, 0:W], in_=msq[:, 0:W], func=AF.Sqrt), reads=["msq"], writes=["msq"])
    S.op("dve", lambda e: e.reciprocal(out=rstd[:, 0:W], in_=msq[:, 0:W]), reads=["msq"], writes=["rstd"])
    for c in range(KC):
        S.op("dve", lambda e: e.tensor_tensor(out=xt[:, c, cs], in0=xt[:, c, cs], in1=mean[:, 0:W], op=ALU.subtract),
             reads=[("xt", c, g), "mean"], writes=[("xt", c, g)])
        S.op("dve", lambda e: e.tensor_tensor(out=xt[:, c, cs], in0=xt[:, c, cs], in1=rstd[:, 0:W], op=ALU.mult),
             reads=[("xt", c, g), "rstd"], writes=[("xt", c, g)])
        S.op("act", lambda e: e.activation(out=xt[:, c, cs], in_=xt[:, c, cs], func=AF.Identity,
                                           bias=vec[:, bcol + c:bcol + c + 1], scale=vec[:, gcol + c:gcol + c + 1]),
             reads=[("xt", c, g), "vec"], writes=[("xt", c, g)])
        if hb is not None:
            S.op("act", lambda e: e.activation(out=hb[:, c, cs], in_=xt[:, c, cs], func=AF.Identity,
                                               bias=vec[:, shcol + c:shcol + c + 1], scale=vec[:, sccol + c:sccol + c + 1]),
                 reads=[("xt", c, g), "vec"], writes=[(hbkey, c, g)])


ADA_N = 3 * D // NCORE


def build_ada():
    C = Ctx()
    nc, S = C.nc, C.S
    cT = C.din("cT", [128, KC, B])
    w = C.din("w", [4, D, ADA_N])
    bias = C.din("bias", [B, 4 * ADA_N])
    o = C.dout("o", [B, 4 * ADA_N])
    ct = C.sb([128, KC, B], F32, "ct")
    ca = C.sb([128, KC, B], F32, "ca")
    bt = C.sb([B, 4 * ADA_N], F32, "bt")
    ot = C.sb([B, 4 * ADA_N], F32, "ot")
    wts = [C.sb([128, KC, 512], F32, "w") for _ in range(2)]
    wrot = Rot("w", wts, [S.dma_sem("w") for _ in range(2)])
    pss = Rot("ps", [C.ps() for _ in range(2)])
    ds = S.dma_sem("m")
    S.dma("sp", ct[:], cT[:, :, :], writes=["ct"], sem=ds)
    S.dma("sp", bt[:], bias[:, :], writes=["bt"], sem=ds)
    S.op("act", lambda e: e.activation(out=ca[:], in_=ct[:], func=AF.Silu), reads=["ct"], writes=["ca"])
    for m in range(4):
        wv = w[m].rearrange("(kc p) n -> p kc n", p=128)
        for t in range(ADA_N // 512):
            wb, wkey, wsem = wrot.next()
            S.dma("sp", wb[:], wv[:, :, t * 512:(t + 1) * 512], writes=[wkey], sem=wsem)
            pt, pkey, _ = pss.next()
            for k in range(KC):
                S.op("pe", lambda e: e.matmul(pt[0:B, :], lhsT=ca[:, k, :], rhs=wb[:, k, :], start=(k == 0), stop=(k == KC - 1)),
                     reads=[wkey, "ca"], writes=[pkey], mark=(k == KC - 1))
            off = m * ADA_N + t * 512
            S.op("dve", lambda e: e.tensor_tensor(out=ot[:, off:off + 512], in0=pt[0:B, :], in1=bt[:, off:off + 512], op=ALU.add),
                 reads=[pkey, "bt"], writes=["ot"])
    S.dma("sp", o[:, :], ot[:], reads=["ot"], sem=ds)
    return C.close()


PRE_VEC = dict(sc=0, sh=32, nbf=64, n=65)


def build_pre0():
    C = Ctx()
    nc, S = C.nc, C.S
    xT = C.din("xT", [D, TPC])
    vecd = C.din("vec", [128, PRE_VEC["n"]])
    wq = C.din("wq", [96, 128, KC, 128])
    wf = C.din("wf", [128, KC, 32])
    qkvT = C.dout("qkvT", [96, 128, TPC])
    logfT = C.dout("logfT", [32, TPC])
    xv = xT.rearrange("(c p) t -> p c t", p=128)
    vec = C.sb([128, PRE_VEC["n"]], F32, "vec")
    xt = C.sb([128, KC, TT], F32, "xt")
    hb = C.sb([128, KC, TT], BF16, "hb")
    wfb = C.sb([128, KC, 32], BF16, "wfb")
    wrot = Rot("w", [C.sb([128, KC, 128], BF16, "w") for _ in range(3)], [S.dma_sem("w") for _ in range(3)])
    pss = Rot("ps", [C.ps() for _ in range(4)])
    obr = Rot("ob", [C.sb([128, TT], F32, "ob") for _ in range(4)], [S.dma_sem("o") for _ in range(4)])
    fb = [C.sb([32, TT], F32, "fb") for _ in range(2)]
    ds = S.dma_sem("m")
    xs = S.dma_sem("x")
    S.dma("sp", vec[:], vecd[:, :], writes=["vec"], sem=ds)
    S.dma("pool", wfb[:], wf[:, :, :], writes=["wfb"], sem=ds)
    sc1 = C.sb([128, KC], F32, "sc1")
    S.op("dve", lambda e: e.tensor_scalar(out=sc1[:], in0=vec[:, 0:32], scalar1=1.0, scalar2=None, op0=ALU.add),
         reads=["vec"], writes=["sc1"])
    qscale = 128.0 ** -0.5
    hb2 = [hb, C.sb([128, KC, TT], BF16, "hb1")]
    for tp in range(TPC // (2 * TT)):
        colsl = []
        for hf in range(2):
            t = 2 * tp + hf
            cols = slice(t * TT, (t + 1) * TT)
            colsl.append(cols)
            S.dma("sp", xt[:], xv[:, :, cols], writes=[("xt", c, 0) for c in range(KC)], sem=xs)
            for c in range(KC):
                S.op("act", lambda e: e.activation(out=hb2[hf][:, c, :], in_=xt[:, c, :], func=AF.Identity,
                                                   bias=vec[:, 32 + c:33 + c], scale=sc1[:, c:c + 1]),
                     reads=[("xt", c, 0), "vec", "sc1"], writes=[("hb", hf, c)])
        for oc in range(96):
            wb = wkey = None
            for hf in range(2):
                pt, pkey, _ = pss.next()
                if wb is None:
                    wb, wkey = emit_proj(C, wq[oc], wrot, KC, 128, lambda k: hb2[hf][:, k, :], lambda k: [("hb", hf, k)], pt, pkey, TT)
                else:
                    for k in range(KC):
                        S.op("pe", lambda e: e.matmul(pt[:, :], lhsT=wb[:, k, :], rhs=hb2[hf][:, k, :], start=(k == 0), stop=(k == KC - 1)),
                             reads=[wkey, ("hb", hf, k)], writes=[pkey], mark=(k == KC - 1))
                ob, okey, osem = obr.next()
                sc = qscale if oc < 32 else 1.0
                if hf == 0:
                    S.op("act", lambda e: e.activation(out=ob[:], in_=pt[:], func=AF.Copy, scale=sc), reads=[pkey], writes=[okey])
                else:
                    S.op("dve", lambda e: e.tensor_scalar(out=ob[:], in0=pt[:], scalar1=sc, scalar2=None, op0=ALU.mult),
                         reads=[pkey], writes=[okey])
                S.dma("sp", qkvT[oc][:, colsl[hf]], ob[:], reads=[okey], sem=osem)
        for hf in range(2):
            pt, pkey, _ = pss.next()
            for k in range(KC):
                S.op("pe", lambda e: e.matmul(pt[0:32, :], lhsT=wfb[:, k, :], rhs=hb2[hf][:, k, :], start=(k == 0), stop=(k == KC - 1)),
                     reads=["wfb", ("hb", hf, k)], writes=[pkey], mark=(k == KC - 1))
            S.op("act", lambda e: e.activation(out=fb[0][:], in_=pt[0:32, :], func=AF.Exp, bias=vec[0:32, 64:65], scale=-1.0),
                 reads=[pkey, "vec"], writes=["fb0"])
            S.op("act", lambda e: e.activation(out=fb[0][:], in_=fb[0][:], func=AF.Ln, bias=1.0, scale=1.0),
                 reads=["fb0"], writes=["fb0"])
            S.op("dve", lambda e: e.tensor_scalar(out=fb[1][:], in0=fb[0][:], scalar1=-1.0, scalar2=None, op0=ALU.mult),
                 reads=["fb0"], writes=["fb1"])
            S.dma("sp", logfT[:, colsl[hf]], fb[1][:], reads=["fb1"], sem=ds)
    return C.close()


NH = 8
QT = 512
NQT = SEQ // QT
NKT = SEQ // 128


def build_att(kind):
    fox = (kind == "fox")
    LA = 2
    C = Ctx()
    nc, S = C.nc, C.S
    qT = C.din("qT", [NH, 128, SEQ])
    maskd = C.din("mask", [4, 128, QT])
    oT = C.dout("oT", [NH, 128, SEQ])
    if fox:
        kT = C.din("kT", [NH, 128, SEQ])
        vtok = C.din("vtok", [NH, SEQ, 128])
        logf = C.din("logf", [NH, SEQ])
        cumd = C.dscratch("cumd", [NH, SEQ])
    else:
        qrT = C.din("qrT", [NH, 64, SEQ])
        ckvT = C.din("ckvT", [512, SEQ])
        krT = C.din("krT", [64, SEQ])
        wk = C.din("wk", [NH, 128, 4, 128])
        wv = C.din("wv", [NH, 128, 4, 128])
    ones = C.sb([128, 128], BF16, "ones")
    S.op("pool", lambda e: e.memset(ones[:], 1.0), writes=["ones"])
    mask = C.sb([128, 4, QT], F32, "mask")
    ds = S.dma_sem("m")
    for o in range(4):
        S.dma("sp", mask[:, o, :], maskd[o], writes=["mask"], sem=ds)
    NBQ = 2 if fox else 1
    Qb = [C.sb([128, SEQ], BF16, "Qb") for _ in range(NBQ)]
    Kb = [C.sb([128, SEQ], BF16, "Kb") for _ in range(2)]
    Vb = [C.sb([128, NKT, 128], BF16, "Vb") for _ in range(2)]
    hsq = [S.dma_sem("hq") for _ in range(2)]
    hsk = [S.dma_sem("hk") for _ in range(2)]
    hsv = [S.dma_sem("hv") for _ in range(2)]
    psS = Rot("psS", [C.ps() for _ in range(4)])
    psO = Rot("psO", [C.ps() for _ in range(2)])
    psLr = Rot("psLs", [C.ps() for _ in range(2)])
    ptr = Rot("pt", [C.sb([128, QT], BF16, "pt") for _ in range(4)])
    tfr = Rot("tf", [C.sb([128, QT], F32, "tf") for _ in range(3 if fox else 1)])
    obr = Rot("ob", [C.sb([128, QT], F32, "ob") for _ in range(2)], [S.dma_sem("o") for _ in range(2)])
    rinv = C.sb([128, QT], F32, "rinv")
    if fox:
        SC = 2048
        lf = C.sb([NH, SC], F32, "lf")
        onesf = C.sb([NH, SC], F32, "onesf")
        cum = [C.sb([NH, SC], F32, "cum") for _ in range(2)]
        S.op("pool", lambda e: e.memset(onesf[:], 1.0), writes=["onesf"])
        for sg_ in range(SEQ // SC):
            sl = slice(sg_ * SC, (sg_ + 1) * SC)
            cb_, cprev = cum[sg_ % 2], cum[(sg_ + 1) % 2]
            S.dma("sp", lf[:], logf[:, sl], writes=["lf"], sem=ds)
            init = 0.0 if sg_ == 0 else cprev[:, SC - 1:SC]
            S.op("dve", lambda e: e.tensor_tensor_scan(out=cb_[:], data0=onesf[:], data1=lf[:], initial=init,
                                                       op0=ALU.mult, op1=ALU.add),
                 reads=["lf", "onesf", ("cum", (sg_ + 1) % 2)], writes=[("cum", sg_ % 2)])
            S.dma("sp", cumd[:, sl], cb_[:], reads=[("cum", sg_ % 2)], writes=["cumd"], sem=ds)
        cqb = C.sb([128, SEQ], F32, "cqb")
        nck = [C.sb([128, NKT], F32, "nck") for _ in range(2)]
        cqm = C.sb([128, 4, QT], F32, "cqm")
        cqs = S.dma_sem("cq")
    else:
        QRb = C.sb([64, SEQ], BF16, "QRb")
        KRb = C.sb([64, SEQ], BF16, "KRb")
        accs = [C.sb([128, QT], F32, "acc") for _ in range(2)]
        onesf32 = C.sb([128, 128], F32, "onesf32")
        S.op("pool", lambda e: e.memset(onesf32[:], 1.0), writes=["onesf32"])
        Cb = C.sb([128, 4, SEQ], BF16, "Cb")
        wkb = [C.sb([128, 4, 128], BF16, "wkb") for _ in range(2)]
        wvb = [C.sb([128, 4, 128], BF16, "wvb") for _ in range(2)]
        S.dma("pool", KRb[:], krT[:, :], writes=["KRb"], sem=ds)
        for cc in range(4):
            S.dma("pool", Cb[:, cc, :], ckvT[cc * 128:(cc + 1) * 128, :], writes=["Cb"], sem=ds)

    def prologue(h):
        hb_ = h % 2
        if fox:
            S.dma("pool", Qb[hb_][:], qT[h], writes=[("Qb", hb_)], sem=hsq[hb_])
            S.dma("pool", Kb[hb_][:], kT[h], writes=[("Kb", hb_)], sem=hsk[hb_])
            S.dma("pool", Vb[hb_][:], vtok[h].rearrange("(t p) d -> p t d", p=128), writes=[("Vb", hb_)], sem=hsv[hb_])
            with nc.allow_non_contiguous_dma(reason="cum column layout"):
                S.dma("sp", nck[hb_][:], cumd[h].rearrange("(t p) -> p t", p=128), reads=["cumd"], writes=[("nck", hb_)], sem=hsk[hb_])
            S.op("dve", lambda e: e.tensor_scalar(out=nck[hb_][:], in0=nck[hb_][:], scalar1=-1.0, scalar2=None, op0=ALU.mult),
                 reads=[("nck", hb_)], writes=[("nck", hb_)])
        else:
            S.dma("pool", wkb[hb_][:], wk[h], writes=[("wkb", hb_)], sem=hsk[hb_])
            S.dma("pool", wvb[hb_][:], wv[h], writes=[("wvb", hb_)], sem=hsv[hb_])
            for t in range(NQT):
                pt_, pkey, _ = psS.next()
                for cc in range(4):
                    S.op("pe", lambda e: e.matmul(pt_[:, :], lhsT=wkb[hb_][:, cc, :], rhs=Cb[:, cc, t * QT:(t + 1) * QT],
                                                  start=(cc == 0), stop=(cc == 3)),
                         reads=[("wkb", hb_), "Cb"], writes=[pkey], mark=(cc == 3))
                S.op("act", lambda e: e.activation(out=Kb[hb_][:, t * QT:(t + 1) * QT], in_=pt_[:, :], func=AF.Copy),
                     reads=[pkey], writes=[("Kb", hb_)])
            for kt4 in range(NKT // 4):
                pt_, pkey, _ = psS.next()
                for j in range(4):
                    kt = kt4 * 4 + j
                    for cc in range(4):
                        S.op("pe", lambda e: e.matmul(pt_[:, j * 128:(j + 1) * 128], lhsT=Cb[:, cc, kt * 128:(kt + 1) * 128],
                                                      rhs=wvb[hb_][:, cc, :], start=(cc == 0), stop=(cc == 3)),
                             reads=[("wvb", hb_), "Cb"], writes=[pkey], mark=(cc == 3 and j == 3))
                S.op("dve", lambda e: e.tensor_copy(out=Vb[hb_][:, kt4 * 4:(kt4 + 1) * 4, :],
                                                    in_=pt_[:, :].rearrange("p (j d) -> p j d", j=4)),
                     reads=[pkey], writes=[("Vb", hb_)])

    prologue(0)
    for h in range(NH):
        hb_ = h % 2
        qb_ = h % NBQ
        Q, Kt, V = Qb[qb_], Kb[hb_], Vb[hb_]
        qkey, kkey, vkey = ("Qb", qb_), ("Kb", hb_), ("Vb", hb_)
        if fox:
            S.dma("sp", cqb[:], cumd[h:h + 1, :].partition_broadcast(128), reads=["cumd"], writes=["cqb"], sem=cqs)
        else:
            S.dma("pool", Q[:], qT[h], writes=[qkey], sem=hsq[0])
            S.dma("pool", QRb[:], qrT[h], writes=["QRb"], sem=hsq[1])
        if h + 1 < NH:
            prologue(h + 1)
        blocks = [(qt, kt) for qt in range(NQT) for kt in range(4 * qt + 4)]
        st = {}
        qst = {}

        def stage_a(i):
            qt, kt = blocks[i]
            qs = slice(qt * QT, (qt + 1) * QT)
            ks = slice(kt * 128, (kt + 1) * 128)
            diag = kt - 4 * qt
            if kt == 0:
                if fox:
                    for o in range(4):
                        S.op("pool", lambda e: e.tensor_tensor(out=cqm[:, o, :], in0=cqb[:, qs], in1=mask[:, o, :], op=ALU.add),
                             reads=["cqb", "mask"], writes=[("cqm", o)])
                qst[qt] = (psO.next(), psLr.next())
            pS, pskey, _ = psS.next()
            if fox:
                S.op("pe", lambda e: e.matmul(pS[:, :], lhsT=Kt[:, ks], rhs=Q[:, qs], start=True, stop=True),
                     reads=[kkey, qkey], writes=[pskey])
            else:
                S.op("pe", lambda e: e.matmul(pS[:, :], lhsT=Kt[:, ks], rhs=Q[:, qs], start=True, stop=False),
                     reads=[kkey, qkey], writes=[pskey], mark=False)
                S.op("pe", lambda e: e.matmul(pS[:, :], lhsT=KRb[:, ks], rhs=QRb[:, qs], start=False, stop=True),
                     reads=["KRb", "QRb"], writes=[pskey])
            pt_, ptkey, _ = ptr.next()
            if fox:
                tf, tfkey, _ = tfr.next()
                if diag >= 0:
                    S.op("dve", lambda e: e.tensor_tensor(out=tf[:], in0=pS[:, :], in1=cqm[:, diag, :], op=ALU.add),
                         reads=[pskey, ("cqm", diag)], writes=[tfkey])
                else:
                    S.op("dve", lambda e: e.tensor_tensor(out=tf[:], in0=pS[:, :], in1=cqb[:, qs], op=ALU.add),
                         reads=[pskey, "cqb"], writes=[tfkey])
                S.op("act", lambda e: e.activation(out=pt_[:], in_=tf[:], func=AF.Exp, bias=nck[hb_][:, kt:kt + 1], scale=1.0),
                     reads=[tfkey, ("nck", hb_)], writes=[ptkey])
            else:
                if diag >= 0:
                    tf, tfkey, _ = tfr.next()
                    S.op("dve", lambda e: e.tensor_tensor(out=tf[:], in0=pS[:, :], in1=mask[:, diag, :], op=ALU.add),
                         reads=[pskey, "mask"], writes=[tfkey])
                    S.op("act", lambda e: e.activation(out=pt_[:], in_=tf[:], func=AF.Exp), reads=[tfkey], writes=[ptkey])
                else:
                    S.op("act", lambda e: e.activation(out=pt_[:], in_=pS[:, :], func=AF.Exp), reads=[pskey], writes=[ptkey])
            st[i] = (pt_, ptkey)

        def stage_b(i):
            qt, kt = blocks[i]
            qs = slice(qt * QT, (qt + 1) * QT)
            nkt = 4 * qt + 4
            (po, pokey, _), (pl, plkey, _) = qst[qt]
            pt_, ptkey = st.pop(i)
            last = (kt == nkt - 1)
            S.op("pe", lambda e: e.matmul(po[:, :], lhsT=V[:, kt, :], rhs=pt_[:], start=(kt == 0), stop=last),
                 reads=[vkey, ptkey], writes=[pokey], mark=last)
            if fox:
                S.op("pe", lambda e: e.matmul(pl[:, :], lhsT=ones[:, :], rhs=pt_[:], start=(kt == 0), stop=last),
                     reads=["ones", ptkey], writes=[plkey], mark=True)
            else:
                ac, ackey = accs[qt % 2], ("acc", qt % 2)
                if kt == 0:
                    S.op("dve", lambda e: e.tensor_copy(out=ac[:], in_=pt_[:]), reads=[ptkey], writes=[ackey])
                else:
                    S.op("dve", lambda e: e.tensor_tensor(out=ac[:], in0=ac[:], in1=pt_[:], op=ALU.add),
                         reads=[ptkey, ackey], writes=[ackey])
                if last:
                    S.op("pe", lambda e: e.matmul(pl[:, :], lhsT=onesf32[:, :], rhs=ac[:], start=True, stop=True),
                         reads=["onesf32", ackey], writes=[plkey], mark=True)
            if last:
                S.op("dve", lambda e: e.reciprocal(out=rinv[:], in_=pl[:, :]), reads=[plkey], writes=["rinv"])
                ob, okey, osem = obr.next()
                S.op("dve", lambda e: e.tensor_tensor(out=ob[:], in0=po[:, :], in1=rinv[:], op=ALU.mult),
                     reads=[pokey, "rinv"], writes=[okey])
                S.dma("sp", oT[h][:, qs], ob[:], reads=[okey], sem=osem)
                del qst[qt]

        nb = len(blocks)
        for i in range(nb + LA):
            if i < nb:
                stage_a(i)
            if i - LA >= 0:
                stage_b(i - LA)
    return C.close()


def post_vec_layout(extras):
    names = [("gate1", 32), ("lng0", 32), ("lnb0", 32), ("sc2", 32), ("sh2", 32), ("gate2", 32),
             ("lng1", 32), ("lnb1", 32), ("cw0", 172), ("cw1", 172), ("cw2", 172), ("cb", 172), ("hflag", 1)]
    if extras:
        names += [("sc3", 32), ("sh3", 32), ("kvn", 4), ("qn", 8)]
    off = {}
    o = 0
    for n, w in names:
        off[n] = o
        o += w
    off["n"] = o
    return off


def build_post(extras):
    L = post_vec_layout(extras)
    C = Ctx()
    nc, S = C.nc, C.S
    NCOL = TPC + 2
    xT = C.din("xT", [D, NCOL])
    aT = C.din("aT", [D, NCOL])
    vecd = C.din("vec", [128, L["n"]])
    wo = C.din("wo", [KC, 128, KC, 128])
    wup = C.din("wup", [2 * FC, 128, KC, 128])
    wdn = C.din("wdn", [4, KC, 128, FQ, 128])
    xo = C.dout("xo", [D, TPC])
    xv = xT.rearrange("(c p) t -> p c t", p=128)
    av = aT.rearrange("(c p) t -> p c t", p=128)
    xov = xo.rearrange("(c p) t -> p c t", p=128)
    if extras:
        ropeC = C.din("ropeC", [64, TPC])
        ropeS = C.din("ropeS", [64, TPC])
        wdkv = C.din("wdkv", [6, 128, KC, 128])
        wdq = C.din("wdq", [8, 128, KC, 128])
        wuq = C.din("wuq", [32, 3, 128, 8, 128])
        ckvT = C.dout("ckvT", [512, TPC])
        krT = C.dout("krT", [64, TPC])
        qnT = C.dout("qnT", [32, 128, TPC])
        qrT = C.dout("qrT", [32, 64, TPC])

    NV = L["n"]
    vec2 = C.sb([128, NV + 128], F32, "vec2")
    vec = vec2
    dv = vec2[:, NV:NV + 128]
    ones = C.sb([128, 128], BF16, "ones")
    S.op("pool", lambda e: e.memset(ones[:], 1.0), writes=["ones"])
    ds = S.dma_sem("m")
    S.dma("sp", vec2[:, 0:NV], vecd[:, :], writes=["vec"], sem=ds)
    S.op("dve", lambda e: e.tensor_scalar(out=dv[:, 0:32], in0=vec[:, L["gate1"]:L["gate1"] + 32], scalar1=1.0, scalar2=1.0 / ALPHA,
                                          op0=ALU.add, op1=ALU.mult), reads=["vec"], writes=["vec"])
    S.op("dve", lambda e: e.tensor_scalar(out=dv[:, 32:64], in0=vec[:, L["gate2"]:L["gate2"] + 32], scalar1=1.0, scalar2=1.0 / ALPHA,
                                          op0=ALU.add, op1=ALU.mult), reads=["vec"], writes=["vec"])
    S.op("dve", lambda e: e.tensor_scalar(out=dv[:, 64:96], in0=vec[:, L["sc2"]:L["sc2"] + 32], scalar1=1.0, scalar2=None,
                                          op0=ALU.add), reads=["vec"], writes=["vec"])
    if extras:
        S.op("dve", lambda e: e.tensor_scalar(out=dv[:, 96:128], in0=vec[:, L["sc3"]:L["sc3"] + 32], scalar1=1.0, scalar2=None,
                                              op0=ALU.add), reads=["vec"], writes=["vec"])
    G1, G2, SC2, SC3 = NV, NV + 32, NV + 64, NV + 96

    XW = TT + 2
    xt = C.sb([128, KC, XW], F32, "xt")
    hb = C.sb([128, KC, XW], BF16, "hb")
    act = C.sb([128, FQ, TT], BF16, "act")
    uh = C.sb([128, 2 * FC, 2], F32, "uh")
    wrot = Rot("w", [C.sb([128, KC, 128], BF16, "w") for _ in range(3)], [S.dma_sem("w") for _ in range(3)])
    psA = Rot("psA", [C.ps() for _ in range(2)])
    psU = Rot("psU", [C.ps() for _ in range(4)])
    psL = [C.ps(), C.ps()]
    K = dict(vec=vec2, ones=ones, psL=psL,
             tmpb=Rot("tmpb", [C.sb([128, TT], BF16, "tmpb") for _ in range(2)]),
             st=[C.sb([128, TT], F32, "st") for _ in range(3)])
    ubr = Rot("ub", [C.sb([128, XW], F32, "ub") for _ in range(3)], [S.dma_sem("ub") for _ in range(3)])
    cvr = Rot("cv", [C.sb([128, TT], F32, "cv") for _ in range(4)], [S.dma_sem("cv") for _ in range(4)])
    sgr = Rot("sg", [C.sb([128, TT], F32, "sg") for _ in range(2)])
    xs = S.dma_sem("x")
    os_ = S.dma_sem("o")
    S.op("pool", lambda e: e.memset(uh[:], 0.0), writes=["uh"])

    for t in range(TPC // TT):
        groups = [(1, slice(2, XW), TT)]
        if t == 0:
            groups = [(0, slice(0, 2), 2)] + groups
        c0 = 2 + t * TT
        lo = 0 if t == 0 else c0
        so = 0 if t == 0 else 2
        gk = [0, 1] if t == 0 else [1]
        S.dma("sp", xt[:, :, so:XW], xv[:, :, lo:c0 + TT], writes=[("xt", c, g) for c in range(KC) for g in gk], sem=xs)
        S.dma("pool", hb[:, :, so:XW], av[:, :, lo:c0 + TT], writes=[("hb", c, g) for c in range(KC) for g in gk], sem=xs)
        for dc in range(KC):
            wb = wkey = None
            for (g, cs, W) in groups:
                pt, pkey, _ = psA.next()
                if wb is None:
                    wb, wkey = emit_proj(C, wo[dc], wrot, KC, 128, lambda k: hb[:, k, cs], lambda k: [("hb", k, g)], pt, pkey, W)
                else:
                    for k in range(KC):
                        S.op("pe", lambda e: e.matmul(pt[:, 0:W], lhsT=wb[:, k, :], rhs=hb[:, k, cs], start=(k == 0), stop=(k == KC - 1)),
                             reads=[wkey, ("hb", k, g)], writes=[pkey], mark=(k == KC - 1))
                S.op("dve", lambda e: e.scalar_tensor_tensor(out=xt[:, dc, cs], in0=pt[:, 0:W], scalar=vec2[:, G1 + dc:G1 + dc + 1],
                                                             in1=xt[:, dc, cs], op0=ALU.mult, op1=ALU.add),
                     reads=[pkey, ("xt", dc, g), "vec"], writes=[("xt", dc, g)])
        for (g, cs, W) in groups:
            emit_ln(C, K, xt, g, cs, W, L["lng0"], L["lnb0"], SC2, L["sh2"], hb, "hb")
        for half in range(4):
            nq = QB[half + 1] - QB[half]
            for j in range(nq):
                jj = QB[half] + j
                cvs = []
                for which, ci in (("a", jj), ("g", FC + jj)):
                    wb = wkey = None
                    if t == 0:
                        pt, pkey, _ = psA.next()
                        wb, wkey = emit_proj(C, wup[ci], wrot, KC, 128, lambda k: hb[:, k, 0:2], lambda k: [("hb", k, 0)], pt, pkey, 2)
                        S.op("dve", lambda e: e.tensor_scalar(out=uh[:, ci, :], in0=pt[:, 0:2], scalar1=vec2[:, L["hflag"]:L["hflag"] + 1],
                                                              scalar2=None, op0=ALU.mult),
                             reads=[pkey, "vec"], writes=[("uh", ci)])
                    pt, pkey, _ = psU.next()
                    if wb is None:
                        wb, wkey = emit_proj(C, wup[ci], wrot, KC, 128, lambda k: hb[:, k, 2:XW], lambda k: [("hb", k, 1)], pt, pkey, TT)
                    else:
                        for k in range(KC):
                            S.op("pe", lambda e: e.matmul(pt[:, :], lhsT=wb[:, k, :], rhs=hb[:, k, 2:XW], start=(k == 0), stop=(k == KC - 1)),
                                 reads=[wkey, ("hb", k, 1)], writes=[pkey], mark=(k == KC - 1))
                    ub, ukey, _ = ubr.next()
                    S.op("act", lambda e: e.activation(out=ub[:, 2:XW], in_=pt[:, :], func=AF.Copy), reads=[pkey], writes=[ukey])
                    S.op("act", lambda e: e.activation(out=ub[:, 0:2], in_=uh[:, ci, :], func=AF.Copy), reads=[("uh", ci)], writes=[ukey])
                    S.op("act", lambda e: e.activation(out=uh[:, ci, :], in_=ub[:, TT:XW], func=AF.Copy), reads=[ukey], writes=[("uh", ci)])
                    cv, ckey, _ = cvr.next()
                    S.op("dve", lambda e: e.tensor_scalar(out=cv[:], in0=ub[:, 2:XW], scalar1=vec2[:, L["cw2"] + ci:L["cw2"] + ci + 1],
                                                          scalar2=vec2[:, L["cb"] + ci:L["cb"] + ci + 1], op0=ALU.mult, op1=ALU.add),
                         reads=[ukey, "vec"], writes=[ckey])
                    S.op("dve", lambda e: e.scalar_tensor_tensor(out=cv[:], in0=ub[:, 1:XW - 1], scalar=vec2[:, L["cw1"] + ci:L["cw1"] + ci + 1],
                                                                 in1=cv[:], op0=ALU.mult, op1=ALU.add),
                         reads=[ukey, ckey, "vec"], writes=[ckey])
                    S.op("dve", lambda e: e.scalar_tensor_tensor(out=cv[:], in0=ub[:, 0:TT], scalar=vec2[:, L["cw0"] + ci:L["cw0"] + ci + 1],
                                                                 in1=cv[:], op0=ALU.mult, op1=ALU.add),
                         reads=[ukey, ckey, "vec"], writes=[ckey])
                    cvs.append((cv, ckey))
                sg, sgkey, _ = sgr.next()
                S.op("act", lambda e: e.activation(out=sg[:], in_=cvs[1][0][:], func=AF.Silu), reads=[cvs[1][1]], writes=[sgkey])
                S.op("dve", lambda e: e.tensor_tensor(out=act[:, j, :], in0=cvs[0][0][:], in1=sg[:], op=ALU.mult),
                     reads=[cvs[0][1], sgkey], writes=[("act", j)])
            for dc in range(KC):
                pt, pkey, _ = psA.next()
                emit_proj(C, wdn[half, dc], wrot, nq, 128, lambda k: act[:, k, :], lambda k: [("act", k)], pt, pkey, TT)
                S.op("dve", lambda e: e.scalar_tensor_tensor(out=xt[:, dc, 2:XW], in0=pt[:, :], scalar=vec2[:, G2 + dc:G2 + dc + 1],
                                                             in1=xt[:, dc, 2:XW], op0=ALU.mult, op1=ALU.add),
                     reads=[pkey, ("xt", dc, 1), "vec"], writes=[("xt", dc, 1)])
        cs = slice(2, XW)
        emit_ln(C, K, xt, 1, cs, TT, L["lng1"], L["lnb1"], 0, 0, None, None)
        ocols = slice(t * TT, (t + 1) * TT)
        S.dma("sp", xov[:, :, ocols], xt[:, :, 2:XW], reads=[("xt", c, 1) for c in range(KC)], sem=os_)
        if extras:
            emit_mla_extras(C, K, L, dict(xt=xt, hb=hb, act=act, vec2=vec2, ones=ones, wrot=wrot, psA=psA, psU=psU,
                                          ocols=ocols, XW=XW, ckvT=ckvT, krT=krT, qnT=qnT, qrT=qrT, ropeC=ropeC, ropeS=ropeS,
                                          wdkv=wdkv, wdq=wdq, wuq=wuq, SC3=SC3, cvr=cvr, ubr=ubr))
    return C.close()


def emit_mla_extras(C, K, L, V):
    S = C.S
    xt, hb, act, vec2, ones, wrot, psA, psU = (V[k] for k in ("xt", "hb", "act", "vec2", "ones", "wrot", "psA", "psU"))
    ocols, XW, SC3, cvr, ubr = V["ocols"], V["XW"], V["SC3"], V["cvr"], V["ubr"]
    ckvT, krT, qnT, qrT, ropeC, ropeS, wdkv, wdq, wuq = (V[k] for k in ("ckvT", "krT", "qnT", "qrT", "ropeC", "ropeS", "wdkv", "wdq", "wuq"))
    E = K.get("_extras")
    if E is None:
        E = K["_extras"] = dict(qdn=C.sb([128, 8, TT], BF16, "qdn"), rc=C.sb([64, TT], F32, "rc"), rs=C.sb([64, TT], F32, "rs"),
                                ds=S.dma_sem("em"))
    qdn, rc, rs = E["qdn"], E["rc"], E["rs"]
    latv = act[:, :, :].rearrange("p a b -> p (a b)").bitcast(F32)
    lat = lambda c: latv[:, c * TT:(c + 1) * TT]
    latk = lambda c: [("act", 2 * c), ("act", 2 * c + 1)]
    S.dma("sp", rc[:], ropeC[:, ocols], writes=["rc"], sem=E["ds"])
    S.dma("sp", rs[:], ropeS[:, ocols], writes=["rs"], sem=E["ds"])
    for c in range(KC):
        S.op("act", lambda e: e.activation(out=hb[:, c, 2:XW], in_=xt[:, c, 2:XW], func=AF.Copy),
             reads=[("xt", c, 1)], writes=[("hb", c, 1)])
    rstd = K["st"][2]

    def rms(nch, gcol, out_fn, out_keys, after=None):
        emit_colsum(C, ones, lat, latk, nch, TT, K["psL"][1], "psL1", True, K["tmpb"])
        S.op("dve", lambda e: e.tensor_scalar(out=rstd[:], in0=K["psL"][1][:, :], scalar1=1.0 / (128 * nch), scalar2=RMS_EPS,
                                              op0=ALU.mult, op1=ALU.add), reads=["psL1"], writes=["rstd"])
        S.op("act", lambda e: e.activation(out=rstd[:], in_=rstd[:], func=AF.Sqrt), reads=["rstd"], writes=["rstd"])
        S.op("dve", lambda e: e.reciprocal(out=rstd[:], in_=rstd[:]), reads=["rstd"], writes=["rstd"])
        for c in range(nch):
            o, okeys = out_fn(c)
            S.op("dve", lambda e: e.scalar_tensor_tensor(out=o, in0=lat(c), scalar=vec2[:, gcol + c:gcol + c + 1],
                                                         in1=rstd[:], op0=ALU.mult, op1=ALU.mult),
                 reads=latk(c) + ["rstd", "vec"], writes=okeys)
            if after is not None:
                after(c, okeys)

    def rope_pair(wsrc_r, wsrc_rot, nk, rhs_fn, rhs_keys, scale, dst):
        outs = []
        for wsrc in (wsrc_r, wsrc_rot):
            pt, pkey, _ = psU.next()
            emit_proj(C, wsrc, wrot, nk, 64, rhs_fn, rhs_keys, pt, pkey, TT)
            outs.append((pt, pkey))
        a, akey, asem = cvr.next()
        b, bkey, _ = cvr.next()
        S.op("dve", lambda e: e.scalar_tensor_tensor(out=a[0:64, :], in0=outs[0][0][0:64, :], scalar=scale, in1=rc[:], op0=ALU.mult, op1=ALU.mult),
             reads=[outs[0][1], "rc"], writes=[akey])
        S.op("dve", lambda e: e.scalar_tensor_tensor(out=b[0:64, :], in0=outs[1][0][0:64, :], scalar=scale, in1=rs[:], op0=ALU.mult, op1=ALU.mult),
             reads=[outs[1][1], "rs"], writes=[bkey])
        S.op("dve", lambda e: e.tensor_tensor(out=a[0:64, :], in0=a[0:64, :], in1=b[0:64, :], op=ALU.add),
             reads=[akey, bkey], writes=[akey])
        S.dma("sp", dst, a[0:64, :], reads=[akey], sem=asem)

    for cc in range(4):
        pt, pkey, _ = psA.next()
        emit_proj(C, wdkv[cc], wrot, KC, 128, lambda k: hb[:, k, 2:XW], lambda k: [("hb", k, 1)], pt, pkey, TT)
        S.op("act", lambda e: e.activation(out=lat(cc), in_=pt[:, :], func=AF.Copy), reads=[pkey], writes=latk(cc))
    cur = {}

    def kv_out(c):
        ob, okey, osem = ubr.next()
        cur["ob"], cur["sem"] = ob, osem
        return ob[:, 0:TT], [okey]

    def kv_after(c, okeys):
        S.dma("sp", ckvT[c * 128:(c + 1) * 128, ocols], cur["ob"][:, 0:TT], reads=okeys, sem=cur["sem"])
    rms(4, L["kvn"], kv_out, None, kv_after)
    rope_pair(wdkv[4], wdkv[5], KC, lambda k: hb[:, k, 2:XW], lambda k: [("hb", k, 1)], 1.0, krT[:, ocols])
    for c in range(KC):
        S.op("act", lambda e: e.activation(out=hb[:, c, 2:XW], in_=xt[:, c, 2:XW], func=AF.Identity,
                                           bias=vec2[:, L["sh3"] + c:L["sh3"] + c + 1], scale=vec2[:, SC3 + c:SC3 + c + 1]),
             reads=[("xt", c, 1), "vec"], writes=[("hb", c, 1)])
    for cc in range(8):
        pt, pkey, _ = psA.next()
        emit_proj(C, wdq[cc], wrot, KC, 128, lambda k: hb[:, k, 2:XW], lambda k: [("hb", k, 1)], pt, pkey, TT)
        S.op("act", lambda e: e.activation(out=lat(cc), in_=pt[:, :], func=AF.Copy), reads=[pkey], writes=latk(cc))
    rms(8, L["qn"], lambda c: (qdn[:, c, :], [("qdn", c)]), None)
    qscale = 192.0 ** -0.5
    for h in range(32):
        pt, pkey, _ = psA.next()
        emit_proj(C, wuq[h, 0], wrot, 8, 128, lambda k: qdn[:, k, :], lambda k: [("qdn", k)], pt, pkey, TT)
        ob, okey, osem = ubr.next()
        S.op("act", lambda e: e.activation(out=ob[:, 0:TT], in_=pt[:, :], func=AF.Copy, scale=qscale), reads=[pkey], writes=[okey])
        S.dma("sp", qnT[h][:, ocols], ob[:, 0:TT], reads=[okey], sem=osem)
        rope_pair(wuq[h, 1], wuq[h, 2], 8, lambda k: qdn[:, k, :], lambda k: [("qdn", k)], qscale, qrT[h][:, ocols])


_PROGS = {}


def _prog(name, fn, *a):
    key = (name,) + a
    if key not in _PROGS:
        _PROGS[key] = fn(*a)
    return _PROGS[key]


def _run(nc, in_maps):
    res = run_bass_kernel_spmd(nc, in_maps, core_ids=list(range(NCORE)))
    return res.results


def _chunks(w, ncols=128):
    Kd, N = w.shape
    return np.ascontiguousarray(w.reshape(Kd // 128, 128, N // ncols, ncols).transpose(2, 1, 0, 3))


def _pcol(v):
    return np.ascontiguousarray(v.reshape(-1, 128).T)


def _masks(kind):
    m = np.zeros((4, 128, QT), np.float32)
    k = np.arange(128)[:, None]
    q = np.arange(QT)[None, :]
    for o in range(4):
        kk = o * 128 + k
        if kind == "fox":
            bad = kk > q
        else:
            bad = (kk // 64) > (q // 64)
        m[o][np.broadcast_to(bad, (128, QT))] = NEG
    return m


def _rope_tables():
    inv = 10000.0 ** (-np.arange(0, 64, 2, dtype=np.float32) / np.float32(64))
    ang = np.arange(SEQ, dtype=np.float32)[:, None] * inv[None, :].astype(np.float32)
    cos = np.cos(ang).astype(np.float32).T
    sin = np.sin(ang).astype(np.float32).T
    Cc = np.concatenate([cos, cos], 0)
    Ss = np.concatenate([-sin, sin], 0)
    return np.ascontiguousarray(Cc), np.ascontiguousarray(Ss)


def _pad_cols(w, n=128):
    out = np.zeros(w.shape[:-1] + (n,), np.float32)
    out[..., :w.shape[-1]] = w
    return out


def _post_inputs(L, xfull, afull, ada_l, ln_g, ln_b, w_o, w_up, conv_w, conv_b, w_down, extras=None):
    lay = post_vec_layout(extras is not None)
    wo = _chunks(w_o)
    wup = _chunks(w_up)
    wdc = _chunks(w_down)
    wd = np.zeros((4, KC, 128, FQ, 128), np.float32)
    for qi in range(4):
        nq = QB[qi + 1] - QB[qi]
        wd[qi, :, :, 0:nq, :] = wdc[:, :, QB[qi]:QB[qi + 1], :]
    maps = []
    for core in range(NCORE):
        b, q = divmod(core, NCORE // B)
        t0 = q * TPC
        vec = np.zeros((128, lay["n"]), np.float32)

        def put(name, arr):
            vec[:, lay[name]:lay[name] + arr.shape[1]] = arr
        a0, a1 = ada_l[0][b], ada_l[1][b]
        put("gate1", _pcol(a0[2 * D:3 * D]))
        put("lng0", _pcol(ln_g[0])); put("lnb0", _pcol(ln_b[0]))
        put("sc2", _pcol(a1[0 * D + D:2 * D])); put("sh2", _pcol(a1[0:D])); put("gate2", _pcol(a1[2 * D:3 * D]))
        put("lng1", _pcol(ln_g[1])); put("lnb1", _pcol(ln_b[1]))
        for j in range(3):
            put("cw%d" % j, _pcol(conv_w[j]))
        put("cb", _pcol(conv_b))
        vec[:, lay["hflag"]] = 0.0 if q == 0 else 1.0
        xT = np.zeros((D, TPC + 2), np.float32)
        aT = np.zeros((D, TPC + 2), np.float32)
        lo = max(t0 - 2, 0)
        xT[:, 2 - (t0 - lo):] = xfull[b, lo:t0 + TPC].T
        aT[:, 2 - (t0 - lo):] = afull[b, lo:t0 + TPC].T
        m = dict(xT=xT, aT=aT, wo=wo, wup=wup, wdn=wd)
        if extras is not None:
            put("sc3", _pcol(extras["ada_next"][b][D:2 * D])); put("sh3", _pcol(extras["ada_next"][b][0:D]))
            put("kvn", _pcol(extras["kv_norm"])); put("qn", _pcol(extras["q_norm"]))
            m.update(ropeC=np.ascontiguousarray(extras["ropeC"][:, t0:t0 + TPC]),
                     ropeS=np.ascontiguousarray(extras["ropeS"][:, t0:t0 + TPC]),
                     wdkv=extras["wdkv"], wdq=extras["wdq"], wuq=extras["wuq"])
        m["vec"] = vec
        maps.append(m)
    return maps


def kernel(x, c, ada_w, ada_b, ln_g, ln_b, fox_w_qkv, fox_w_f, fox_b_f, fox_w_o,
           mla_w_dq, mla_q_norm, mla_w_uq, mla_w_o, mla_w_dkv, mla_kv_norm, mla_w_ukv,
           ffn_w_up, ffn_conv_w, ffn_conv_b, ffn_w_down):
    f = lambda a: np.asarray(a, dtype=np.float32)
    x, c, ada_w, ada_b, ln_g, ln_b = f(x), f(c), f(ada_w), f(ada_b), f(ln_g), f(ln_b)
    HPB = NCORE // B
    cT = np.ascontiguousarray(c.T.reshape(KC, 128, B).transpose(1, 0, 2))
    aw = ada_w.reshape(4, D, 3 * D)
    ab = ada_b.reshape(4, 3 * D)
    maps = []
    for i in range(NCORE):
        cols = slice(i * ADA_N, (i + 1) * ADA_N)
        bias = np.ascontiguousarray(np.broadcast_to(ab[:, cols].reshape(1, 4 * ADA_N), (B, 4 * ADA_N)))
        maps.append(dict(cT=cT, w=np.ascontiguousarray(aw[:, :, cols]), bias=bias))
    res = _run(_prog("ada", build_ada), maps)
    ada = np.concatenate([r["o"].reshape(B, 4, ADA_N) for r in res], axis=2)
    ada = ada.transpose(1, 0, 2).reshape(2, 2, B, 3 * D)

    wq = _chunks(f(fox_w_qkv)[0])
    wf = np.ascontiguousarray(f(fox_w_f)[0].reshape(KC, 128, 32).transpose(1, 0, 2))
    maps = []
    for core in range(NCORE):
        b, q = divmod(core, HPB)
        vec = np.zeros((128, PRE_VEC["n"]), np.float32)
        a0 = ada[0, 0, b]
        vec[:, 0:32] = _pcol(a0[D:2 * D])
        vec[:, 32:64] = _pcol(a0[0:D])
        vec[0:32, 64] = -f(fox_b_f)[0]
        maps.append(dict(xT=np.ascontiguousarray(x[b, q * TPC:(q + 1) * TPC].T), vec=vec, wq=wq, wf=wf))
    res = _run(_prog("pre0", build_pre0), maps)
    qkvT = np.stack([np.concatenate([res[b * HPB + q]["qkvT"] for q in range(HPB)], axis=2) for b in range(B)])
    logfT = np.stack([np.concatenate([res[b * HPB + q]["logfT"] for q in range(HPB)], axis=1) for b in range(B)])

    mk = _masks("fox")
    maps = []
    for core in range(NCORE):
        b, hg = divmod(core, HPB)
        hsl = slice(hg * NH, (hg + 1) * NH)
        vt = np.ascontiguousarray(qkvT[b, 64 + hg * NH:64 + (hg + 1) * NH].transpose(0, 2, 1))
        maps.append(dict(qT=np.ascontiguousarray(qkvT[b, hsl]), kT=np.ascontiguousarray(qkvT[b, 32 + hg * NH:32 + (hg + 1) * NH]),
                         vtok=vt, logf=np.ascontiguousarray(logfT[b, hsl]), mask=mk))
    res = _run(_prog("att", build_att, "fox"), maps)
    att = np.stack([np.concatenate([res[b * HPB + hg]["oT"] for hg in range(HPB)], axis=0) for b in range(B)])
    att_tok = att.reshape(B, D, SEQ).transpose(0, 2, 1)

    ropeC, ropeS = _rope_tables()
    wdkv_full = f(mla_w_dkv)
    perm = np.concatenate([np.arange(32, 64), np.arange(0, 32)])
    kr_w = wdkv_full[:, 512:576]
    wdkv = np.concatenate([_chunks(wdkv_full[:, :512]), _chunks(_pad_cols(kr_w)), _chunks(_pad_cols(kr_w[:, perm]))], axis=0)
    wdq = _chunks(f(mla_w_dq)[0])
    wuq_full = f(mla_w_uq)[0].reshape(1024, 32, 192)
    wuq = np.zeros((32, 3, 128, 8, 128), np.float32)
    for h in range(32):
        wh = wuq_full[:, h, :]
        wuq[h, 0] = _chunks(wh[:, :128])[0]
        wuq[h, 1] = _chunks(_pad_cols(wh[:, 128:192]))[0]
        wuq[h, 2] = _chunks(_pad_cols(wh[:, 128:192][:, perm]))[0]
    extras = dict(ada_next=ada[1, 0], kv_norm=f(mla_kv_norm), q_norm=f(mla_q_norm)[0], ropeC=ropeC, ropeS=ropeS,
                  wdkv=wdkv, wdq=wdq, wuq=wuq)
    maps = _post_inputs(0, x, att_tok, ada[0][:, :, :], ln_g[0], ln_b[0], f(fox_w_o)[0], f(ffn_w_up)[0], f(ffn_conv_w)[0],
                        f(ffn_conv_b)[0], f(ffn_w_down)[0], extras)
    res = _run(_prog("post", build_post, True), maps)
    x1 = np.stack([np.concatenate([res[b * HPB + q]["xo"] for q in range(HPB)], axis=1).T for b in range(B)])
    cat = lambda name, ax: [np.concatenate([res[b * HPB + q][name] for q in range(HPB)], axis=ax) for b in range(B)]
    ckvT, krT, qnT, qrT = cat("ckvT", 1), cat("krT", 1), cat("qnT", 2), cat("qrT", 2)

    mk = _masks("mla")
    wukv = f(mla_w_ukv).reshape(512, 32, 256)
    maps = []
    for core in range(NCORE):
        b, hg = divmod(core, HPB)
        hsl = slice(hg * NH, (hg + 1) * NH)
        wk = np.stack([_chunks(wukv[:, h, :128])[0] for h in range(hg * NH, (hg + 1) * NH)])
        wv = np.stack([_chunks(wukv[:, h, 128:])[0] for h in range(hg * NH, (hg + 1) * NH)])
        maps.append(dict(qT=np.ascontiguousarray(qnT[b][hsl]), qrT=np.ascontiguousarray(qrT[b][hsl]), ckvT=ckvT[b], krT=krT[b],
                         wk=wk, wv=wv, mask=mk))
    res = _run(_prog("att", build_att, "mla"), maps)
    att = np.stack([np.concatenate([res[b * HPB + hg]["oT"] for hg in range(HPB)], axis=0) for b in range(B)])
    att_tok = att.reshape(B, D, SEQ).transpose(0, 2, 1)

    maps = _post_inputs(1, x1, att_tok, ada[1][:, :, :], ln_g[1], ln_b[1], f(mla_w_o)[0], f(ffn_w_up)[1], f(ffn_conv_w)[1],
                        f(ffn_conv_b)[1], f(ffn_w_down)[1], None)
    res = _run(_prog("post", build_post, False), maps)
    out = np.stack([np.concatenate([res[b * HPB + q]["xo"] for q in range(HPB)], axis=1).T for b in range(B)])
    return np.ascontiguousarray(out.astype(np.float32))
```

```python
from contextlib import ExitStack
import numpy as np
import concourse.bass as bass
import concourse.mybir as mybir
from concourse.bass_utils import run_bass_kernel_spmd

F32 = mybir.dt.float32
BF16 = mybir.dt.bfloat16
AF = mybir.ActivationFunctionType
ALU = mybir.AluOpType

D = 4096
B = 2
SEQ = 8192
NCORE = 8
TPC = 2048
TT = 512
KC = D // 128
DFF = 11008
FC = DFF // 128
FQ = 22
QB = [0, 22, 44, 65, 86]
ALPHA = (2.0 * 2) ** 0.25
LN_EPS = 1e-5 / (ALPHA * ALPHA)
RMS_EPS = 1e-6
NEG = -30000.0


class _Sem:
    def __init__(self, handle, step):
        self.h = handle
        self.step = step
        self.count = 0


class _Res:
    __slots__ = ("w", "r")

    def __init__(self):
        self.w = None
        self.r = {}


class Sched:
    def __init__(self, nc, stack):
        self.nc = nc
        self.stack = stack
        self.res = {}
        self.engs = {}
        self.nsem = 0
        for name, e in (("pe", nc.tensor), ("act", nc.scalar), ("dve", nc.vector),
                        ("pool", nc.gpsimd), ("sp", nc.sync)):
            sem = _Sem(self._newsem("e_" + name), 1)
            self.engs[name] = dict(e=e, sem=sem, seen={}, name=name)

    def _newsem(self, name):
        self.nsem += 1
        return self.stack.enter_context(self.nc.semaphore(f"{name}_{self.nsem}"))

    def dma_sem(self, name="d"):
        return _Sem(self._newsem(name), 16)

    def _r(self, key):
        r = self.res.get(key)
        if r is None:
            r = self.res[key] = _Res()
        return r

    def _wait_deps(self, E, reads, writes):
        deps = {}
        for k in reads:
            r = self.res.get(k)
            if r is not None and r.w is not None:
                s, v = r.w
                if deps.get(s, 0) < v:
                    deps[s] = v
        for k in writes:
            r = self.res.get(k)
            if r is not None:
                if r.w is not None:
                    s, v = r.w
                    if deps.get(s, 0) < v:
                        deps[s] = v
                for s, v in r.r.items():
                    if deps.get(s, 0) < v:
                        deps[s] = v
        seen = E["seen"]
        for s, v in deps.items():
            if s is E["sem"] and E["name"] == "pe":
                continue
            if s.step == 16:
                v = s.count
            if seen.get(s, 0) >= v:
                continue
            E["e"].wait_ge(s.h, v)
            seen[s] = v

    def _register(self, ev, reads, writes):
        for k in reads:
            rr = self._r(k).r
            if rr.get(ev[0], 0) < ev[1]:
                rr[ev[0]] = ev[1]
        for k in writes:
            r = self._r(k)
            r.w = ev
            r.r = {}

    def op(self, eng, fn, reads=(), writes=(), mark=True):
        E = self.engs[eng]
        self._wait_deps(E, reads, writes)
        ins = fn(E["e"])
        s = E["sem"]
        if mark:
            ins.then_inc(s.h, 1)
            s.count += 1
            ev = (s, s.count)
        else:
            ev = (s, s.count + 1)
        self._register(ev, reads, writes)
        return ins

    def dma(self, queue, out, in_, reads=(), writes=(), sem=None, **kw):
        E = self.engs[queue]
        self._wait_deps(E, reads, writes)
        ins = E["e"].dma_start(out=out, in_=in_, **kw)
        ins.then_inc(sem.h, 16)
        sem.count += 16
        ev = (sem, sem.count)
        self._register(ev, reads, writes)
        return ev

    def finish(self, queue="sp"):
        E = self.engs[queue]
        allk = list(self.res.keys())
        self._wait_deps(E, allk, allk)


class Ctx:
    def __init__(self):
        self.nc = bass.Bass("TRN2", target_bir_lowering=False)
        self.stack = ExitStack()
        self.S = Sched(self.nc, self.stack)
        self.n = 0

    def sb(self, shape, dt, name="t"):
        self.n += 1
        return self.stack.enter_context(self.nc.sbuf_tensor(f"{name}_{self.n}", list(shape), dt))

    def ps(self, name="ps", shape=(128, 512)):
        self.n += 1
        return self.stack.enter_context(self.nc.psum_tensor(f"{name}_{self.n}", list(shape), F32))

    def din(self, name, shape, dt=F32):
        return self.nc.dram_tensor(name, list(shape), dt, kind="ExternalInput").ap()

    def dout(self, name, shape, dt=F32):
        return self.nc.dram_tensor(name, list(shape), dt, kind="ExternalOutput").ap()

    def dscratch(self, name, shape, dt=F32):
        return self.nc.dram_tensor(name, list(shape), dt, kind="Internal").ap()

    def close(self):
        self.S.finish("sp")
        self.stack.close()
        return self.nc


class Rot:
    def __init__(self, name, bufs, sems=None):
        self.name = name
        self.bufs = bufs
        self.sems = sems
        self.i = -1

    def next(self):
        self.i = (self.i + 1) % len(self.bufs)
        return self.bufs[self.i], (self.name, self.i), (self.sems[self.i] if self.sems else None)


def emit_proj(C, wsrc, wrot, nk, M, rhs_fn, rhs_keys, pst, pkey, W, pcols=None):
    S = C.S
    wb, wkey, wsem = wrot.next()
    S.dma("pool", wb[:, 0:nk, 0:M], wsrc[:, 0:nk, 0:M], writes=[wkey], sem=wsem)
    for k in range(nk):
        S.op("pe", lambda e: e.matmul(pst[0:M, 0:W], lhsT=wb[:, k, 0:M], rhs=rhs_fn(k),
                                      start=(k == 0), stop=(k == nk - 1)),
             reads=[wkey] + rhs_keys(k), writes=[pkey], mark=(k == nk - 1))
    return wb, wkey


def emit_colsum(C, ones, src_fn, src_keys, nchunk, W, psum_t, pkey, sq, tmp_rot):
    S = C.S
    for c in range(nchunk):
        tb, tkey, _ = tmp_rot.next()
        S.op("act", lambda e: e.activation(out=tb[:, 0:W], in_=src_fn(c), func=(AF.Square if sq else AF.Copy)),
             reads=src_keys(c), writes=[tkey])
        S.op("pe", lambda e: e.matmul(psum_t[:, 0:W], lhsT=ones[:, :], rhs=tb[:, 0:W],
                                      start=(c == 0), stop=(c == nchunk - 1)),
             reads=[tkey, "ones"], writes=[pkey])


def emit_ln(C, K, xt, g, cs, W, gcol, bcol, sccol, shcol, hb, hbkey):
    S = C.S
    vec = K["vec"]
    emit_colsum(C, K["ones"], lambda c: xt[:, c, cs], lambda c: [("xt", c, g)], KC, W, K["psL"][0], "psL0", False, K["tmpb"])
    emit_colsum(C, K["ones"], lambda c: xt[:, c, cs], lambda c: [("xt", c, g)], KC, W, K["psL"][1], "psL1", True, K["tmpb"])
    mean, msq, rstd = K["st"][0], K["st"][1], K["st"][2]
    S.op("dve", lambda e: e.tensor_scalar(out=mean[:, 0:W], in0=K["psL"][0][:, 0:W], scalar1=1.0 / D, scalar2=None, op0=ALU.mult),
         reads=["psL0"], writes=["mean"])
    S.op("dve", lambda e: e.tensor_tensor(out=msq[:, 0:W], in0=mean[:, 0:W], in1=mean[:, 0:W], op=ALU.mult),
         reads=["mean"], writes=["msq"])
    S.op("dve", lambda e: e.scalar_tensor_tensor(out=msq[:, 0:W], in0=K["psL"][1][:, 0:W], scalar=1.0 / D, in1=msq[:, 0:W],
                                                 op0=ALU.mult, op1=ALU.subtract),
         reads=["psL1", "msq"], writes=["msq"])
    S.op("dve", lambda e: e.tensor_scalar(out=msq[:, 0:W], in0=msq[:, 0:W], scalar1=LN_EPS, scalar2=None, op0=ALU.add),
         reads=["msq"], writes=["msq"])
    S.op("act", lambda e: e.activation(out=msq[:, 0:W], in_=msq[:, 0:W], func=AF.Sqrt), reads=["msq"], writes=["msq"])
    S.op("dve", lambda e: e.reciprocal(out=rstd[:, 0:W], in_=msq[:, 0:W]), reads=["msq"], writes=["rstd"])
    for c in range(KC):
        S.op("dve", lambda e: e.tensor_tensor(out=xt[:, c, cs], in0=xt[:, c, cs], in1=mean[:, 0:W], op=ALU.subtract),
             reads=[("xt", c, g), "mean"], writes=[("xt", c, g)])
        S.op("dve", lambda e: e.tensor_tensor(out=xt[:, c, cs], in0=xt[:, c, cs], in1=rstd[:, 0:W], op=ALU.mult),
             reads=[("xt", c, g), "rstd"], writes=[("xt", c, g)])
        S.op("act", lambda e: e.activation(out=xt[:, c, cs], in_=xt[:, c, cs], func=AF.Identity,
                                           bias=vec[:, bcol + c:bcol + c + 1], scale=vec[:, gcol + c:gcol + c + 1]),
             reads=[("xt", c, g), "vec"], writes=[("xt", c, g)])
        if hb is not None:
            S.op("act", lambda e: e.activation(out=hb[:, c, cs], in_=xt[:, c, cs], func=AF.Identity,
                                               bias=vec[:, shcol + c:shcol + c + 1], scale=vec[:, sccol + c:sccol + c + 1]),
                 reads=[("xt", c, g), "vec"], writes=[(hbkey, c, g)])


ADA_N = 3 * D // NCORE


def build_ada():
    C = Ctx()
    nc, S = C.nc, C.S
    cT = C.din("cT", [128, KC, B])
    w = C.din("w", [4, D, ADA_N])
    bias = C.din("bias", [B, 4 * ADA_N])
    o = C.dout("o", [B, 4 * ADA_N])
    ct = C.sb([128, KC, B], F32, "ct")
    ca = C.sb([128, KC, B], F32, "ca")
    bt = C.sb([B, 4 * ADA_N], F32, "bt")
    ot = C.sb([B, 4 * ADA_N], F32, "ot")
    wts = [C.sb([128, KC, 512], F32, "w") for _ in range(2)]
    wrot = Rot("w", wts, [S.dma_sem("w") for _ in range(2)])
    pss = Rot("ps", [C.ps() for _ in range(2)])
    ds = S.dma_sem("m")
    S.dma("sp", ct[:], cT[:, :, :], writes=["ct"], sem=ds)
    S.dma("sp", bt[:], bias[:, :], writes=["bt"], sem=ds)
    S.op("act", lambda e: e.activation(out=ca[:], in_=ct[:], func=AF.Silu), reads=["ct"], writes=["ca"])
    for m in range(4):
        wv = w[m].rearrange("(kc p) n -> p kc n", p=128)
        for t in range(ADA_N // 512):
            wb, wkey, wsem = wrot.next()
            S.dma("sp", wb[:], wv[:, :, t * 512:(t + 1) * 512], writes=[wkey], sem=wsem)
            pt, pkey, _ = pss.next()
            for k in range(KC):
                S.op("pe", lambda e: e.matmul(pt[0:B, :], lhsT=ca[:, k, :], rhs=wb[:, k, :], start=(k == 0), stop=(k == KC - 1)),
                     reads=[wkey, "ca"], writes=[pkey], mark=(k == KC - 1))
            off = m * ADA_N + t * 512
            S.op("dve", lambda e: e.tensor_tensor(out=ot[:, off:off + 512], in0=pt[0:B, :], in1=bt[:, off:off + 512], op=ALU.add),
                 reads=[pkey, "bt"], writes=["ot"])
    S.dma("sp", o[:, :], ot[:], reads=["ot"], sem=ds)
    return C.close()


PRE_VEC = dict(sc=0, sh=32, nbf=64, n=65)


def build_pre0():
    C = Ctx()
    nc, S = C.nc, C.S
    xT = C.din("xT", [D, TPC])
    vecd = C.din("vec", [128, PRE_VEC["n"]])
    wq = C.din("wq", [96, 128, KC, 128])
    wf = C.din("wf", [128, KC, 32])
    qkvT = C.dout("qkvT", [96, 128, TPC])
    logfT = C.dout("logfT", [32, TPC])
    xv = xT.rearrange("(c p) t -> p c t", p=128)
    vec = C.sb([128, PRE_VEC["n"]], F32, "vec")
    xt = C.sb([128, KC, TT], F32, "xt")
    hb = C.sb([128, KC, TT], BF16, "hb")
    wfb = C.sb([128, KC, 32], BF16, "wfb")
    wrot = Rot("w", [C.sb([128, KC, 128], BF16, "w") for _ in range(3)], [S.dma_sem("w") for _ in range(3)])
    pss = Rot("ps", [C.ps() for _ in range(4)])
    obr = Rot("ob", [C.sb([128, TT], F32, "ob") for _ in range(4)], [S.dma_sem("o") for _ in range(4)])
    fb = [C.sb([32, TT], F32, "fb") for _ in range(2)]
    ds = S.dma_sem("m")
    xs = S.dma_sem("x")
    S.dma("sp", vec[:], vecd[:, :], writes=["vec"], sem=ds)
    S.dma("pool", wfb[:], wf[:, :, :], writes=["wfb"], sem=ds)
    sc1 = C.sb([128, KC], F32, "sc1")
    S.op("dve", lambda e: e.tensor_scalar(out=sc1[:], in0=vec[:, 0:32], scalar1=1.0, scalar2=None, op0=ALU.add),
         reads=["vec"], writes=["sc1"])
    qscale = 128.0 ** -0.5
    hb2 = [hb, C.sb([128, KC, TT], BF16, "hb1")]
    for tp in range(TPC // (2 * TT)):
        colsl = []
        for hf in range(2):
            t = 2 * tp + hf
            cols = slice(t * TT, (t + 1) * TT)
            colsl.append(cols)
            S.dma("sp", xt[:], xv[:, :, cols], writes=[("xt", c, 0) for c in range(KC)], sem=xs)
            for c in range(KC):
                S.op("act", lambda e: e.activation(out=hb2[hf][:, c, :], in_=xt[:, c, :], func=AF.Identity,
                                                   bias=vec[:, 32 + c:33 + c], scale=sc1[:, c:c + 1]),
                     reads=[("xt", c, 0), "vec", "sc1"], writes=[("hb", hf, c)])
        for oc in range(96):
            wb = wkey = None
            for hf in range(2):
                pt, pkey, _ = pss.next()
                if wb is None:
                    wb, wkey = emit_proj(C, wq[oc], wrot, KC, 128, lambda k: hb2[hf][:, k, :], lambda k: [("hb", hf, k)], pt, pkey, TT)
                else:
                    for k in range(KC):
                        S.op("pe", lambda e: e.matmul(pt[:, :], lhsT=wb[:, k, :], rhs=hb2[hf][:, k, :], start=(k == 0), stop=(k == KC - 1)),
                             reads=[wkey, ("hb", hf, k)], writes=[pkey], mark=(k == KC - 1))
                ob, okey, osem = obr.next()
                sc = qscale if oc < 32 else 1.0
                if hf == 0:
                    S.op("act", lambda e: e.activation(out=ob[:], in_=pt[:], func=AF.Copy, scale=sc), reads=[pkey], writes=[okey])
                else:
                    S.op("dve", lambda e: e.tensor_scalar(out=ob[:], in0=pt[:], scalar1=sc, scalar2=None, op0=ALU.mult),
                         reads=[pkey], writes=[okey])
                S.dma("sp", qkvT[oc][:, colsl[hf]], ob[:], reads=[okey], sem=osem)
        for hf in range(2):
            pt, pkey, _ = pss.next()
            for k in range(KC):
                S.op("pe", lambda e: e.matmul(pt[0:32, :], lhsT=wfb[:, k, :], rhs=hb2[hf][:, k, :], start=(k == 0), stop=(k == KC - 1)),
                     reads=["wfb", ("hb", hf, k)], writes=[pkey], mark=(k == KC - 1))
            S.op("act", lambda e: e.activation(out=fb[0][:], in_=pt[0:32, :], func=AF.Exp, bias=vec[0:32, 64:65], scale=-1.0),
                 reads=[pkey, "vec"], writes=["fb0"])
            S.op("act", lambda e: e.activation(out=fb[0][:], in_=fb[0][:], func=AF.Ln, bias=1.0, scale=1.0),
                 reads=["fb0"], writes=["fb0"])
            S.op("dve", lambda e: e.tensor_scalar(out=fb[1][:], in0=fb[0][:], scalar1=-1.0, scalar2=None, op0=ALU.mult),
                 reads=["fb0"], writes=["fb1"])
            S.dma("sp", logfT[:, colsl[hf]], fb[1][:], reads=["fb1"], sem=ds)
    return C.close()


NH = 8
QT = 512
NQT = SEQ // QT
NKT = SEQ // 128


def build_att(kind):
    fox = (kind == "fox")
    LA = 2
    C = Ctx()
    nc, S = C.nc, C.S
    qT = C.din("qT", [NH, 128, SEQ])
    maskd = C.din("mask", [4, 128, QT])
    oT = C.dout("oT", [NH, 128, SEQ])
    if fox:
        kT = C.din("kT", [NH, 128, SEQ])
        vtok = C.din("vtok", [NH, SEQ, 128])
        logf = C.din("logf", [NH, SEQ])
        cumd = C.dscratch("cumd", [NH, SEQ])
    else:
        qrT = C.din("qrT", [NH, 64, SEQ])
        ckvT = C.din("ckvT", [512, SEQ])
        krT = C.din("krT", [64, SEQ])
        wk = C.din("wk", [NH, 128, 4, 128])
        wv = C.din("wv", [NH, 128, 4, 128])
    ones = C.sb([128, 128], BF16, "ones")
    S.op("pool", lambda e: e.memset(ones[:], 1.0), writes=["ones"])
    mask = C.sb([128, 4, QT], F32, "mask")
    ds = S.dma_sem("m")
    for o in range(4):
        S.dma("sp", mask[:, o, :], maskd[o], writes=["mask"], sem=ds)
    NBQ = 2 if fox else 1
    Qb = [C.sb([128, SEQ], BF16, "Qb") for _ in range(NBQ)]
    Kb = [C.sb([128, SEQ], BF16, "Kb") for _ in range(2)]
    Vb = [C.sb([128, NKT, 128], BF16, "Vb") for _ in range(2)]
    hsq = [S.dma_sem("hq") for _ in range(2)]
    hsk = [S.dma_sem("hk") for _ in range(2)]
    hsv = [S.dma_sem("hv") for _ in range(2)]
    psS = Rot("psS", [C.ps() for _ in range(4)])
    psO = Rot("psO", [C.ps() for _ in range(2)])
    psLr = Rot("psLs", [C.ps() for _ in range(2)])
    ptr = Rot("pt", [C.sb([128, QT], BF16, "pt") for _ in range(4)])
    tfr = Rot("tf", [C.sb([128, QT], F32, "tf") for _ in range(3)])
    obr = Rot("ob", [C.sb([128, QT], F32, "ob") for _ in range(2)], [S.dma_sem("o") for _ in range(2)])
    rinv = C.sb([128, QT], F32, "rinv")
    if fox:
        SC = 2048
        lf = C.sb([NH, SC], F32, "lf")
        onesf = C.sb([NH, SC], F32, "onesf")
        cum = [C.sb([NH, SC], F32, "cum") for _ in range(2)]
        S.op("pool", lambda e: e.memset(onesf[:], 1.0), writes=["onesf"])
        for sg_ in range(SEQ // SC):
            sl = slice(sg_ * SC, (sg_ + 1) * SC)
            cb_, cprev = cum[sg_ % 2], cum[(sg_ + 1) % 2]
            S.dma("sp", lf[:], logf[:, sl], writes=["lf"], sem=ds)
            init = 0.0 if sg_ == 0 else cprev[:, SC - 1:SC]
            S.op("dve", lambda e: e.tensor_tensor_scan(out=cb_[:], data0=onesf[:], data1=lf[:], initial=init,
                                                       op0=ALU.mult, op1=ALU.add),
                 reads=["lf", "onesf", ("cum", (sg_ + 1) % 2)], writes=[("cum", sg_ % 2)])
            S.dma("sp", cumd[:, sl], cb_[:], reads=[("cum", sg_ % 2)], writes=["cumd"], sem=ds)
        cqb = C.sb([128, SEQ], F32, "cqb")
        nck = [C.sb([128, NKT], F32, "nck") for _ in range(2)]
        cqm = C.sb([128, 4, QT], F32, "cqm")
        cqs = S.dma_sem("cq")
    else:
        QRb = C.sb([64, SEQ], BF16, "QRb")
        KRb = C.sb([64, SEQ], BF16, "KRb")
        Cb = C.sb([128, 4, SEQ], BF16, "Cb")
        wkb = [C.sb([128, 4, 128], BF16, "wkb") for _ in range(2)]
        wvb = [C.sb([128, 4, 128], BF16, "wvb") for _ in range(2)]
        S.dma("pool", KRb[:], krT[:, :], writes=["KRb"], sem=ds)
        for cc in range(4):
            S.dma("pool", Cb[:, cc, :], ckvT[cc * 128:(cc + 1) * 128, :], writes=["Cb"], sem=ds)

    def prologue(h):
        hb_ = h % 2
        if fox:
            S.dma("pool", Qb[hb_][:], qT[h], writes=[("Qb", hb_)], sem=hsq[hb_])
            S.dma("pool", Kb[hb_][:], kT[h], writes=[("Kb", hb_)], sem=hsk[hb_])
            S.dma("pool", Vb[hb_][:], vtok[h].rearrange("(t p) d -> p t d", p=128), writes=[("Vb", hb_)], sem=hsv[hb_])
            with nc.allow_non_contiguous_dma(reason="cum column layout"):
                S.dma("sp", nck[hb_][:], cumd[h].rearrange("(t p) -> p t", p=128), reads=["cumd"], writes=[("nck", hb_)], sem=hsk[hb_])
            S.op("dve", lambda e: e.tensor_scalar(out=nck[hb_][:], in0=nck[hb_][:], scalar1=-1.0, scalar2=None, op0=ALU.mult),
                 reads=[("nck", hb_)], writes=[("nck", hb_)])
        else:
            S.dma("pool", wkb[hb_][:], wk[h], writes=[("wkb", hb_)], sem=hsk[hb_])
            S.dma("pool", wvb[hb_][:], wv[h], writes=[("wvb", hb_)], sem=hsv[hb_])
            for t in range(NQT):
                pt_, pkey, _ = psS.next()
                for cc in range(4):
                    S.op("pe", lambda e: e.matmul(pt_[:, :], lhsT=wkb[hb_][:, cc, :], rhs=Cb[:, cc, t * QT:(t + 1) * QT],
                                                  start=(cc == 0), stop=(cc == 3)),
                         reads=[("wkb", hb_), "Cb"], writes=[pkey], mark=(cc == 3))
                S.op("act", lambda e: e.activation(out=Kb[hb_][:, t * QT:(t + 1) * QT], in_=pt_[:, :], func=AF.Copy),
                     reads=[pkey], writes=[("Kb", hb_)])
            for kt4 in range(NKT // 4):
                pt_, pkey, _ = psS.next()
                for j in range(4):
                    kt = kt4 * 4 + j
                    for cc in range(4):
                        S.op("pe", lambda e: e.matmul(pt_[:, j * 128:(j + 1) * 128], lhsT=Cb[:, cc, kt * 128:(kt + 1) * 128],
                                                      rhs=wvb[hb_][:, cc, :], start=(cc == 0), stop=(cc == 3)),
                             reads=[("wvb", hb_), "Cb"], writes=[pkey], mark=(cc == 3 and j == 3))
                S.op("dve", lambda e: e.tensor_copy(out=Vb[hb_][:, kt4 * 4:(kt4 + 1) * 4, :],
                                                    in_=pt_[:, :].rearrange("p (j d) -> p j d", j=4)),
                     reads=[pkey], writes=[("Vb", hb_)])

    prologue(0)
    for h in range(NH):
        hb_ = h % 2
        qb_ = h % NBQ
        Q, Kt, V = Qb[qb_], Kb[hb_], Vb[hb_]
        qkey, kkey, vkey = ("Qb", qb_), ("Kb", hb_), ("Vb", hb_)
        if fox:
            S.dma("sp", cqb[:], cumd[h:h + 1, :].partition_broadcast(128), reads=["cumd"], writes=["cqb"], sem=cqs)
        else:
            S.dma("pool", Q[:], qT[h], writes=[qkey], sem=hsq[0])
            S.dma("pool", QRb[:], qrT[h], writes=["QRb"], sem=hsq[1])
        if h + 1 < NH:
            prologue(h + 1)
        blocks = [(qt, kt) for qt in range(NQT) for kt in range(4 * qt + 4)]
        st = {}
        qst = {}

        def stage_a(i):
            qt, kt = blocks[i]
            qs = slice(qt * QT, (qt + 1) * QT)
            ks = slice(kt * 128, (kt + 1) * 128)
            diag = kt - 4 * qt
            if kt == 0:
                if fox:
                    for o in range(4):
                        S.op("pool", lambda e: e.tensor_tensor(out=cqm[:, o, :], in0=cqb[:, qs], in1=mask[:, o, :], op=ALU.add),
                             reads=["cqb", "mask"], writes=[("cqm", o)])
                qst[qt] = (psO.next(), psLr.next())
            pS, pskey, _ = psS.next()
            if fox:
                S.op("pe", lambda e: e.matmul(pS[:, :], lhsT=Kt[:, ks], rhs=Q[:, qs], start=True, stop=True),
                     reads=[kkey, qkey], writes=[pskey])
            else:
                S.op("pe", lambda e: e.matmul(pS[:, :], lhsT=Kt[:, ks], rhs=Q[:, qs], start=True, stop=False),
                     reads=[kkey, qkey], writes=[pskey], mark=False)
                S.op("pe", lambda e: e.matmul(pS[:, :], lhsT=KRb[:, ks], rhs=QRb[:, qs], start=False, stop=True),
                     reads=["KRb", "QRb"], writes=[pskey])
            pt_, ptkey, _ = ptr.next()
            if fox:
                tf, tfkey, _ = tfr.next()
                if diag >= 0:
                    S.op("dve", lambda e: e.tensor_tensor(out=tf[:], in0=pS[:, :], in1=cqm[:, diag, :], op=ALU.add),
                         reads=[pskey, ("cqm", diag)], writes=[tfkey])
                else:
                    S.op("dve", lambda e: e.tensor_tensor(out=tf[:], in0=pS[:, :], in1=cqb[:, qs], op=ALU.add),
                         reads=[pskey, "cqb"], writes=[tfkey])
                S.op("act", lambda e: e.activation(out=pt_[:], in_=tf[:], func=AF.Exp, bias=nck[hb_][:, kt:kt + 1], scale=1.0),
                     reads=[tfkey, ("nck", hb_)], writes=[ptkey])
            else:
                if diag >= 0:
                    tf, tfkey, _ = tfr.next()
                    S.op("dve", lambda e: e.tensor_tensor(out=tf[:], in0=pS[:, :], in1=mask[:, diag, :], op=ALU.add),
                         reads=[pskey, "mask"], writes=[tfkey])
                    S.op("act", lambda e: e.activation(out=pt_[:], in_=tf[:], func=AF.Exp), reads=[tfkey], writes=[ptkey])
                else:
                    S.op("act", lambda e: e.activation(out=pt_[:], in_=pS[:, :], func=AF.Exp), reads=[pskey], writes=[ptkey])
            st[i] = (pt_, ptkey)

        def stage_b(i):
            qt, kt = blocks[i]
            qs = slice(qt * QT, (qt + 1) * QT)
            nkt = 4 * qt + 4
            (po, pokey, _), (pl, plkey, _) = qst[qt]
            pt_, ptkey = st.pop(i)
            last = (kt == nkt - 1)
            S.op("pe", lambda e: e.matmul(po[:, :], lhsT=V[:, kt, :], rhs=pt_[:], start=(kt == 0), stop=last),
                 reads=[vkey, ptkey], writes=[pokey], mark=last)
            S.op("pe", lambda e: e.matmul(pl[:, :], lhsT=ones[:, :], rhs=pt_[:], start=(kt == 0), stop=last),
                 reads=["ones", ptkey], writes=[plkey], mark=True)
            if last:
                S.op("dve", lambda e: e.reciprocal(out=rinv[:], in_=pl[:, :]), reads=[plkey], writes=["rinv"])
                ob, okey, osem = obr.next()
                S.op("dve", lambda e: e.tensor_tensor(out=ob[:], in0=po[:, :], in1=rinv[:], op=ALU.mult),
                     reads=[pokey, "rinv"], writes=[okey])
                S.dma("sp", oT[h][:, qs], ob[:], reads=[okey], sem=osem)
                del qst[qt]

        nb = len(blocks)
        for i in range(nb + LA):
            if i < nb:
                stage_a(i)
            if i - LA >= 0:
                stage_b(i - LA)
    return C.close()


def post_vec_layout(extras):
    names = [("gate1", 32), ("lng0", 32), ("lnb0", 32), ("sc2", 32), ("sh2", 32), ("gate2", 32),
             ("lng1", 32), ("lnb1", 32), ("cw0", 172), ("cw1", 172), ("cw2", 172), ("cb", 172), ("hflag", 1)]
    if extras:
        names += [("sc3", 32), ("sh3", 32), ("kvn", 4), ("qn", 8)]
    off = {}
    o = 0
    for n, w in names:
        off[n] = o
        o += w
    off["n"] = o
    return off


def build_post(extras):
    L = post_vec_layout(extras)
    C = Ctx()
    nc, S = C.nc, C.S
    NCOL = TPC + 2
    xT = C.din("xT", [D, NCOL])
    aT = C.din("aT", [D, NCOL])
    vecd = C.din("vec", [128, L["n"]])
    wo = C.din("wo", [KC, 128, KC, 128])
    wup = C.din("wup", [2 * FC, 128, KC, 128])
    wdn = C.din("wdn", [4, KC, 128, FQ, 128])
    xo = C.dout("xo", [D, TPC])
    xv = xT.rearrange("(c p) t -> p c t", p=128)
    av = aT.rearrange("(c p) t -> p c t", p=128)
    xov = xo.rearrange("(c p) t -> p c t", p=128)
    if extras:
        ropeC = C.din("ropeC", [64, TPC])
        ropeS = C.din("ropeS", [64, TPC])
        wdkv = C.din("wdkv", [6, 128, KC, 128])
        wdq = C.din("wdq", [8, 128, KC, 128])
        wuq = C.din("wuq", [32, 3, 128, 8, 128])
        ckvT = C.dout("ckvT", [512, TPC])
        krT = C.dout("krT", [64, TPC])
        qnT = C.dout("qnT", [32, 128, TPC])
        qrT = C.dout("qrT", [32, 64, TPC])

    NV = L["n"]
    vec2 = C.sb([128, NV + 128], F32, "vec2")
    vec = vec2
    dv = vec2[:, NV:NV + 128]
    ones = C.sb([128, 128], BF16, "ones")
    S.op("pool", lambda e: e.memset(ones[:], 1.0), writes=["ones"])
    ds = S.dma_sem("m")
    S.dma("sp", vec2[:, 0:NV], vecd[:, :], writes=["vec"], sem=ds)
    S.op("dve", lambda e: e.tensor_scalar(out=dv[:, 0:32], in0=vec[:, L["gate1"]:L["gate1"] + 32], scalar1=1.0, scalar2=1.0 / ALPHA,
                                          op0=ALU.add, op1=ALU.mult), reads=["vec"], writes=["vec"])
    S.op("dve", lambda e: e.tensor_scalar(out=dv[:, 32:64], in0=vec[:, L["gate2"]:L["gate2"] + 32], scalar1=1.0, scalar2=1.0 / ALPHA,
                                          op0=ALU.add, op1=ALU.mult), reads=["vec"], writes=["vec"])
    S.op("dve", lambda e: e.tensor_scalar(out=dv[:, 64:96], in0=vec[:, L["sc2"]:L["sc2"] + 32], scalar1=1.0, scalar2=None,
                                          op0=ALU.add), reads=["vec"], writes=["vec"])
    if extras:
        S.op("dve", lambda e: e.tensor_scalar(out=dv[:, 96:128], in0=vec[:, L["sc3"]:L["sc3"] + 32], scalar1=1.0, scalar2=None,
                                              op0=ALU.add), reads=["vec"], writes=["vec"])
    G1, G2, SC2, SC3 = NV, NV + 32, NV + 64, NV + 96

    XW = TT + 2
    xt = C.sb([128, KC, XW], F32, "xt")
    hb = C.sb([128, KC, XW], BF16, "hb")
    act = C.sb([128, FQ, TT], BF16, "act")
    uh = C.sb([128, 2 * FC, 2], F32, "uh")
    wrot = Rot("w", [C.sb([128, KC, 128], BF16, "w") for _ in range(3)], [S.dma_sem("w") for _ in range(3)])
    psA = Rot("psA", [C.ps() for _ in range(2)])
    psU = Rot("psU", [C.ps() for _ in range(4)])
    psL = [C.ps(), C.ps()]
    K = dict(vec=vec2, ones=ones, psL=psL,
             tmpb=Rot("tmpb", [C.sb([128, TT], BF16, "tmpb") for _ in range(2)]),
             st=[C.sb([128, TT], F32, "st") for _ in range(3)])
    ubr = Rot("ub", [C.sb([128, XW], F32, "ub") for _ in range(3)], [S.dma_sem("ub") for _ in range(3)])
    cvr = Rot("cv", [C.sb([128, TT], F32, "cv") for _ in range(4)], [S.dma_sem("cv") for _ in range(4)])
    sgr = Rot("sg", [C.sb([128, TT], F32, "sg") for _ in range(2)])
    xs = S.dma_sem("x")
    os_ = S.dma_sem("o")
    S.op("pool", lambda e: e.memset(uh[:], 0.0), writes=["uh"])

    for t in range(TPC // TT):
        groups = [(1, slice(2, XW), TT)]
        if t == 0:
            groups = [(0, slice(0, 2), 2)] + groups
        c0 = 2 + t * TT
        lo = 0 if t == 0 else c0
        so = 0 if t == 0 else 2
        gk = [0, 1] if t == 0 else [1]
        S.dma("sp", xt[:, :, so:XW], xv[:, :, lo:c0 + TT], writes=[("xt", c, g) for c in range(KC) for g in gk], sem=xs)
        S.dma("pool", hb[:, :, so:XW], av[:, :, lo:c0 + TT], writes=[("hb", c, g) for c in range(KC) for g in gk], sem=xs)
        for dc in range(KC):
            wb = wkey = None
            for (g, cs, W) in groups:
                pt, pkey, _ = psA.next()
                if wb is None:
                    wb, wkey = emit_proj(C, wo[dc], wrot, KC, 128, lambda k: hb[:, k, cs], lambda k: [("hb", k, g)], pt, pkey, W)
                else:
                    for k in range(KC):
                        S.op("pe", lambda e: e.matmul(pt[:, 0:W], lhsT=wb[:, k, :], rhs=hb[:, k, cs], start=(k == 0), stop=(k == KC - 1)),
                             reads=[wkey, ("hb", k, g)], writes=[pkey], mark=(k == KC - 1))
                S.op("dve", lambda e: e.scalar_tensor_tensor(out=xt[:, dc, cs], in0=pt[:, 0:W], scalar=vec2[:, G1 + dc:G1 + dc + 1],
                                                             in1=xt[:, dc, cs], op0=ALU.mult, op1=ALU.add),
                     reads=[pkey, ("xt", dc, g), "vec"], writes=[("xt", dc, g)])
        for (g, cs, W) in groups:
            emit_ln(C, K, xt, g, cs, W, L["lng0"], L["lnb0"], SC2, L["sh2"], hb, "hb")
        for half in range(4):
            nq = QB[half + 1] - QB[half]
            for j in range(nq):
                jj = QB[half] + j
                cvs = []
                for which, ci in (("a", jj), ("g", FC + jj)):
                    wb = wkey = None
                    if t == 0:
                        pt, pkey, _ = psA.next()
                        wb, wkey = emit_proj(C, wup[ci], wrot, KC, 128, lambda k: hb[:, k, 0:2], lambda k: [("hb", k, 0)], pt, pkey, 2)
                        S.op("dve", lambda e: e.tensor_scalar(out=uh[:, ci, :], in0=pt[:, 0:2], scalar1=vec2[:, L["hflag"]:L["hflag"] + 1],
                                                              scalar2=None, op0=ALU.mult),
                             reads=[pkey, "vec"], writes=[("uh", ci)])
                    pt, pkey, _ = psU.next()
                    if wb is None:
                        wb, wkey = emit_proj(C, wup[ci], wrot, KC, 128, lambda k: hb[:, k, 2:XW], lambda k: [("hb", k, 1)], pt, pkey, TT)
                    else:
                        for k in range(KC):
                            S.op("pe", lambda e: e.matmul(pt[:, :], lhsT=wb[:, k, :], rhs=hb[:, k, 2:XW], start=(k == 0), stop=(k == KC - 1)),
                                 reads=[wkey, ("hb", k, 1)], writes=[pkey], mark=(k == KC - 1))
                    ub, ukey, _ = ubr.next()
                    S.op("act", lambda e: e.activation(out=ub[:, 2:XW], in_=pt[:, :], func=AF.Copy), reads=[pkey], writes=[ukey])
                    S.op("act", lambda e: e.activation(out=ub[:, 0:2], in_=uh[:, ci, :], func=AF.Copy), reads=[("uh", ci)], writes=[ukey])
                    S.op("act", lambda e: e.activation(out=uh[:, ci, :], in_=ub[:, TT:XW], func=AF.Copy), reads=[ukey], writes=[("uh", ci)])
                    cv, ckey, _ = cvr.next()
                    S.op("dve", lambda e: e.tensor_scalar(out=cv[:], in0=ub[:, 2:XW], scalar1=vec2[:, L["cw2"] + ci:L["cw2"] + ci + 1],
                                                          scalar2=vec2[:, L["cb"] + ci:L["cb"] + ci + 1], op0=ALU.mult, op1=ALU.add),
                         reads=[ukey, "vec"], writes=[ckey])
                    S.op("dve", lambda e: e.scalar_tensor_tensor(out=cv[:], in0=ub[:, 1:XW - 1], scalar=vec2[:, L["cw1"] + ci:L["cw1"] + ci + 1],
                                                                 in1=cv[:], op0=ALU.mult, op1=ALU.add),
                         reads=[ukey, ckey, "vec"], writes=[ckey])
                    S.op("dve", lambda e: e.scalar_tensor_tensor(out=cv[:], in0=ub[:, 0:TT], scalar=vec2[:, L["cw0"] + ci:L["cw0"] + ci + 1],
                                                                 in1=cv[:], op0=ALU.mult, op1=ALU.add),
                         reads=[ukey, ckey, "vec"], writes=[ckey])
                    cvs.append((cv, ckey))
                sg, sgkey, _ = sgr.next()
                S.op("act", lambda e: e.activation(out=sg[:], in_=cvs[1][0][:], func=AF.Silu), reads=[cvs[1][1]], writes=[sgkey])
                S.op("dve", lambda e: e.tensor_tensor(out=act[:, j, :], in0=cvs[0][0][:], in1=sg[:], op=ALU.mult),
                     reads=[cvs[0][1], sgkey], writes=[("act", j)])
            for dc in range(KC):
                pt, pkey, _ = psA.next()
                emit_proj(C, wdn[half, dc], wrot, nq, 128, lambda k: act[:, k, :], lambda k: [("act", k)], pt, pkey, TT)
                S.op("dve", lambda e: e.scalar_tensor_tensor(out=xt[:, dc, 2:XW], in0=pt[:, :], scalar=vec2[:, G2 + dc:G2 + dc + 1],
                                                             in1=xt[:, dc, 2:XW], op0=ALU.mult, op1=ALU.add),
                     reads=[pkey, ("xt", dc, 1), "vec"], writes=[("xt", dc, 1)])
        cs = slice(2, XW)
        emit_ln(C, K, xt, 1, cs, TT, L["lng1"], L["lnb1"], 0, 0, None, None)
        ocols = slice(t * TT, (t + 1) * TT)
        S.dma("sp", xov[:, :, ocols], xt[:, :, 2:XW], reads=[("xt", c, 1) for c in range(KC)], sem=os_)
        if extras:
            emit_mla_extras(C, K, L, dict(xt=xt, hb=hb, act=act, vec2=vec2, ones=ones, wrot=wrot, psA=psA, psU=psU,
                                          ocols=ocols, XW=XW, ckvT=ckvT, krT=krT, qnT=qnT, qrT=qrT, ropeC=ropeC, ropeS=ropeS,
                                          wdkv=wdkv, wdq=wdq, wuq=wuq, SC3=SC3, cvr=cvr, ubr=ubr))
    return C.close()


def emit_mla_extras(C, K, L, V):
    S = C.S
    xt, hb, act, vec2, ones, wrot, psA, psU = (V[k] for k in ("xt", "hb", "act", "vec2", "ones", "wrot", "psA", "psU"))
    ocols, XW, SC3, cvr, ubr = V["ocols"], V["XW"], V["SC3"], V["cvr"], V["ubr"]
    ckvT, krT, qnT, qrT, ropeC, ropeS, wdkv, wdq, wuq = (V[k] for k in ("ckvT", "krT", "qnT", "qrT", "ropeC", "ropeS", "wdkv", "wdq", "wuq"))
    E = K.get("_extras")
    if E is None:
        E = K["_extras"] = dict(qdn=C.sb([128, 8, TT], BF16, "qdn"), rc=C.sb([64, TT], F32, "rc"), rs=C.sb([64, TT], F32, "rs"),
                                ds=S.dma_sem("em"))
    qdn, rc, rs = E["qdn"], E["rc"], E["rs"]
    latv = act[:, :, :].rearrange("p a b -> p (a b)").bitcast(F32)
    lat = lambda c: latv[:, c * TT:(c + 1) * TT]
    latk = lambda c: [("act", 2 * c), ("act", 2 * c + 1)]
    S.dma("sp", rc[:], ropeC[:, ocols], writes=["rc"], sem=E["ds"])
    S.dma("sp", rs[:], ropeS[:, ocols], writes=["rs"], sem=E["ds"])
    for c in range(KC):
        S.op("act", lambda e: e.activation(out=hb[:, c, 2:XW], in_=xt[:, c, 2:XW], func=AF.Copy),
             reads=[("xt", c, 1)], writes=[("hb", c, 1)])
    rstd = K["st"][2]

    def rms(nch, gcol, out_fn, out_keys, after=None):
        emit_colsum(C, ones, lat, latk, nch, TT, K["psL"][1], "psL1", True, K["tmpb"])
        S.op("dve", lambda e: e.tensor_scalar(out=rstd[:], in0=K["psL"][1][:, :], scalar1=1.0 / (128 * nch), scalar2=RMS_EPS,
                                              op0=ALU.mult, op1=ALU.add), reads=["psL1"], writes=["rstd"])
        S.op("act", lambda e: e.activation(out=rstd[:], in_=rstd[:], func=AF.Sqrt), reads=["rstd"], writes=["rstd"])
        S.op("dve", lambda e: e.reciprocal(out=rstd[:], in_=rstd[:]), reads=["rstd"], writes=["rstd"])
        for c in range(nch):
            o, okeys = out_fn(c)
            S.op("dve", lambda e: e.scalar_tensor_tensor(out=o, in0=lat(c), scalar=vec2[:, gcol + c:gcol + c + 1],
                                                         in1=rstd[:], op0=ALU.mult, op1=ALU.mult),
                 reads=latk(c) + ["rstd", "vec"], writes=okeys)
            if after is not None:
                after(c, okeys)

    def rope_pair(wsrc_r, wsrc_rot, nk, rhs_fn, rhs_keys, scale, dst):
        outs = []
        for wsrc in (wsrc_r, wsrc_rot):
            pt, pkey, _ = psU.next()
            emit_proj(C, wsrc, wrot, nk, 64, rhs_fn, rhs_keys, pt, pkey, TT)
            outs.append((pt, pkey))
        a, akey, asem = cvr.next()
        b, bkey, _ = cvr.next()
        S.op("dve", lambda e: e.scalar_tensor_tensor(out=a[0:64, :], in0=outs[0][0][0:64, :], scalar=scale, in1=rc[:], op0=ALU.mult, op1=ALU.mult),
             reads=[outs[0][1], "rc"], writes=[akey])
        S.op("dve", lambda e: e.scalar_tensor_tensor(out=b[0:64, :], in0=outs[1][0][0:64, :], scalar=scale, in1=rs[:], op0=ALU.mult, op1=ALU.mult),
             reads=[outs[1][1], "rs"], writes=[bkey])
        S.op("dve", lambda e: e.tensor_tensor(out=a[0:64, :], in0=a[0:64, :], in1=b[0:64, :], op=ALU.add),
             reads=[akey, bkey], writes=[akey])
        S.dma("sp", dst, a[0:64, :], reads=[akey], sem=asem)

    for cc in range(4):
        pt, pkey, _ = psA.next()
        emit_proj(C, wdkv[cc], wrot, KC, 128, lambda k: hb[:, k, 2:XW], lambda k: [("hb", k, 1)], pt, pkey, TT)
        S.op("act", lambda e: e.activation(out=lat(cc), in_=pt[:, :], func=AF.Copy), reads=[pkey], writes=latk(cc))
    cur = {}

    def kv_out(c):
        ob, okey, osem = ubr.next()
        cur["ob"], cur["sem"] = ob, osem
        return ob[:, 0:TT], [okey]

    def kv_after(c, okeys):
        S.dma("sp", ckvT[c * 128:(c + 1) * 128, ocols], cur["ob"][:, 0:TT], reads=okeys, sem=cur["sem"])
    rms(4, L["kvn"], kv_out, None, kv_after)
    rope_pair(wdkv[4], wdkv[5], KC, lambda k: hb[:, k, 2:XW], lambda k: [("hb", k, 1)], 1.0, krT[:, ocols])
    for c in range(KC):
        S.op("act", lambda e: e.activation(out=hb[:, c, 2:XW], in_=xt[:, c, 2:XW], func=AF.Identity,
                                           bias=vec2[:, L["sh3"] + c:L["sh3"] + c + 1], scale=vec2[:, SC3 + c:SC3 + c + 1]),
             reads=[("xt", c, 1), "vec"], writes=[("hb", c, 1)])
    for cc in range(8):
        pt, pkey, _ = psA.next()
        emit_proj(C, wdq[cc], wrot, KC, 128, lambda k: hb[:, k, 2:XW], lambda k: [("hb", k, 1)], pt, pkey, TT)
        S.op("act", lambda e: e.activation(out=lat(cc), in_=pt[:, :], func=AF.Copy), reads=[pkey], writes=latk(cc))
    rms(8, L["qn"], lambda c: (qdn[:, c, :], [("qdn", c)]), None)
    qscale = 192.0 ** -0.5
    for h in range(32):
        pt, pkey, _ = psA.next()
        emit_proj(C, wuq[h, 0], wrot, 8, 128, lambda k: qdn[:, k, :], lambda k: [("qdn", k)], pt, pkey, TT)
        ob, okey, osem = ubr.next()
        S.op("act", lambda e: e.activation(out=ob[:, 0:TT], in_=pt[:, :], func=AF.Copy, scale=qscale), reads=[pkey], writes=[okey])
        S.dma("sp", qnT[h][:, ocols], ob[:, 0:TT], reads=[okey], sem=osem)
        rope_pair(wuq[h, 1], wuq[h, 2], 8, lambda k: qdn[:, k, :], lambda k: [("qdn", k)], qscale, qrT[h][:, ocols])


_PROGS = {}


def _prog(name, fn, *a):
    key = (name,) + a
    if key not in _PROGS:
        _PROGS[key] = fn(*a)
    return _PROGS[key]


def _run(nc, in_maps):
    res = run_bass_kernel_spmd(nc, in_maps, core_ids=list(range(NCORE)))
    return res.results


def _chunks(w, ncols=128):
    Kd, N = w.shape
    return np.ascontiguousarray(w.reshape(Kd // 128, 128, N // ncols, ncols).transpose(2, 1, 0, 3))


def _pcol(v):
    return np.ascontiguousarray(v.reshape(-1, 128).T)


def _masks(kind):
    m = np.zeros((4, 128, QT), np.float32)
    k = np.arange(128)[:, None]
    q = np.arange(QT)[None, :]
    for o in range(4):
        kk = o * 128 + k
        if kind == "fox":
            bad = kk > q
        else:
            bad = (kk // 64) > (q // 64)
        m[o][np.broadcast_to(bad, (128, QT))] = NEG
    return m


def _rope_tables():
    inv = 10000.0 ** (-np.arange(0, 64, 2, dtype=np.float32) / np.float32(64))
    ang = np.arange(SEQ, dtype=np.float32)[:, None] * inv[None, :].astype(np.float32)
    cos = np.cos(ang).astype(np.float32).T
    sin = np.sin(ang).astype(np.float32).T
    Cc = np.concatenate([cos, cos], 0)
    Ss = np.concatenate([-sin, sin], 0)
    return np.ascontiguousarray(Cc), np.ascontiguousarray(Ss)


def _pad_cols(w, n=128):
    out = np.zeros(w.shape[:-1] + (n,), np.float32)
    out[..., :w.shape[-1]] = w
    return out


def _post_inputs(L, xfull, afull, ada_l, ln_g, ln_b, w_o, w_up, conv_w, conv_b, w_down, extras=None):
    lay = post_vec_layout(extras is not None)
    wo = _chunks(w_o)
    wup = _chunks(w_up)
    wdc = _chunks(w_down)
    wd = np.zeros((4, KC, 128, FQ, 128), np.float32)
    for qi in range(4):
        nq = QB[qi + 1] - QB[qi]
        wd[qi, :, :, 0:nq, :] = wdc[:, :, QB[qi]:QB[qi + 1], :]
    maps = []
    for core in range(NCORE):
        b, q = divmod(core, NCORE // B)
        t0 = q * TPC
        vec = np.zeros((128, lay["n"]), np.float32)

        def put(name, arr):
            vec[:, lay[name]:lay[name] + arr.shape[1]] = arr
        a0, a1 = ada_l[0][b], ada_l[1][b]
        put("gate1", _pcol(a0[2 * D:3 * D]))
        put("lng0", _pcol(ln_g[0])); put("lnb0", _pcol(ln_b[0]))
        put("sc2", _pcol(a1[0 * D + D:2 * D])); put("sh2", _pcol(a1[0:D])); put("gate2", _pcol(a1[2 * D:3 * D]))
        put("lng1", _pcol(ln_g[1])); put("lnb1", _pcol(ln_b[1]))
        for j in range(3):
            put("cw%d" % j, _pcol(conv_w[j]))
        put("cb", _pcol(conv_b))
        vec[:, lay["hflag"]] = 0.0 if q == 0 else 1.0
        xT = np.zeros((D, TPC + 2), np.float32)
        aT = np.zeros((D, TPC + 2), np.float32)
        lo = max(t0 - 2, 0)
        xT[:, 2 - (t0 - lo):] = xfull[b, lo:t0 + TPC].T
        aT[:, 2 - (t0 - lo):] = afull[b, lo:t0 + TPC].T
        m = dict(xT=xT, aT=aT, wo=wo, wup=wup, wdn=wd)
        if extras is not None:
            put("sc3", _pcol(extras["ada_next"][b][D:2 * D])); put("sh3", _pcol(extras["ada_next"][b][0:D]))
            put("kvn", _pcol(extras["kv_norm"])); put("qn", _pcol(extras["q_norm"]))
            m.update(ropeC=np.ascontiguousarray(extras["ropeC"][:, t0:t0 + TPC]),
                     ropeS=np.ascontiguousarray(extras["ropeS"][:, t0:t0 + TPC]),
                     wdkv=extras["wdkv"], wdq=extras["wdq"], wuq=extras["wuq"])
        m["vec"] = vec
        maps.append(m)
    return maps


def kernel(x, c, ada_w, ada_b, ln_g, ln_b, fox_w_qkv, fox_w_f, fox_b_f, fox_w_o,
           mla_w_dq, mla_q_norm, mla_w_uq, mla_w_o, mla_w_dkv, mla_kv_norm, mla_w_ukv,
           ffn_w_up, ffn_conv_w, ffn_conv_b, ffn_w_down):
    f = lambda a: np.asarray(a, dtype=np.float32)
    x, c, ada_w, ada_b, ln_g, ln_b = f(x), f(c), f(ada_w), f(ada_b), f(ln_g), f(ln_b)
    HPB = NCORE // B
    cT = np.ascontiguousarray(c.T.reshape(KC, 128, B).transpose(1, 0, 2))
    aw = ada_w.reshape(4, D, 3 * D)
    ab = ada_b.reshape(4, 3 * D)
    maps = []
    for i in range(NCORE):
        cols = slice(i * ADA_N, (i + 1) * ADA_N)
        bias = np.ascontiguousarray(np.broadcast_to(ab[:, cols].reshape(1, 4 * ADA_N), (B, 4 * ADA_N)))
        maps.append(dict(cT=cT, w=np.ascontiguousarray(aw[:, :, cols]), bias=bias))
    res = _run(_prog("ada", build_ada), maps)
    ada = np.concatenate([r["o"].reshape(B, 4, ADA_N) for r in res], axis=2)
    ada = ada.transpose(1, 0, 2).reshape(2, 2, B, 3 * D)

    wq = _chunks(f(fox_w_qkv)[0])
    wf = np.ascontiguousarray(f(fox_w_f)[0].reshape(KC, 128, 32).transpose(1, 0, 2))
    maps = []
    for core in range(NCORE):
        b, q = divmod(core, HPB)
        vec = np.zeros((128, PRE_VEC["n"]), np.float32)
        a0 = ada[0, 0, b]
        vec[:, 0:32] = _pcol(a0[D:2 * D])
        vec[:, 32:64] = _pcol(a0[0:D])
        vec[0:32, 64] = -f(fox_b_f)[0]
        maps.append(dict(xT=np.ascontiguousarray(x[b, q * TPC:(q + 1) * TPC].T), vec=vec, wq=wq, wf=wf))
    res = _run(_prog("pre0", build_pre0), maps)
    qkvT = np.stack([np.concatenate([res[b * HPB + q]["qkvT"] for q in range(HPB)], axis=2) for b in range(B)])
    logfT = np.stack([np.concatenate([res[b * HPB + q]["logfT"] for q in range(HPB)], axis=1) for b in range(B)])

    mk = _masks("fox")
    maps = []
    for core in range(NCORE):
        b, hg = divmod(core, HPB)
        hsl = slice(hg * NH, (hg + 1) * NH)
        vt = np.ascontiguousarray(qkvT[b, 64 + hg * NH:64 + (hg + 1) * NH].transpose(0, 2, 1))
        maps.append(dict(qT=np.ascontiguousarray(qkvT[b, hsl]), kT=np.ascontiguousarray(qkvT[b, 32 + hg * NH:32 + (hg + 1) * NH]),
                         vtok=vt, logf=np.ascontiguousarray(logfT[b, hsl]), mask=mk))
    res = _run(_prog("att", build_att, "fox"), maps)
    att = np.stack([np.concatenate([res[b * HPB + hg]["oT"] for hg in range(HPB)], axis=0) for b in range(B)])
    att_tok = att.reshape(B, D, SEQ).transpose(0, 2, 1)

    ropeC, ropeS = _rope_tables()
    wdkv_full = f(mla_w_dkv)
    perm = np.concatenate([np.arange(32, 64), np.arange(0, 32)])
    kr_w = wdkv_full[:, 512:576]
    wdkv = np.concatenate([_chunks(wdkv_full[:, :512]), _chunks(_pad_cols(kr_w)), _chunks(_pad_cols(kr_w[:, perm]))], axis=0)
    wdq = _chunks(f(mla_w_dq)[0])
    wuq_full = f(mla_w_uq)[0].reshape(1024, 32, 192)
    wuq = np.zeros((32, 3, 128, 8, 128), np.float32)
    for h in range(32):
        wh = wuq_full[:, h, :]
        wuq[h, 0] = _chunks(wh[:, :128])[0]
        wuq[h, 1] = _chunks(_pad_cols(wh[:, 128:192]))[0]
        wuq[h, 2] = _chunks(_pad_cols(wh[:, 128:192][:, perm]))[0]
    extras = dict(ada_next=ada[1, 0], kv_norm=f(mla_kv_norm), q_norm=f(mla_q_norm)[0], ropeC=ropeC, ropeS=ropeS,
                  wdkv=wdkv, wdq=wdq, wuq=wuq)
    maps = _post_inputs(0, x, att_tok, ada[0][:, :, :], ln_g[0], ln_b[0], f(fox_w_o)[0], f(ffn_w_up)[0], f(ffn_conv_w)[0],
                        f(ffn_conv_b)[0], f(ffn_w_down)[0], extras)
    res = _run(_prog("post", build_post, True), maps)
    x1 = np.stack([np.concatenate([res[b * HPB + q]["xo"] for q in range(HPB)], axis=1).T for b in range(B)])
    cat = lambda name, ax: [np.concatenate([res[b * HPB + q][name] for q in range(HPB)], axis=ax) for b in range(B)]
    ckvT, krT, qnT, qrT = cat("ckvT", 1), cat("krT", 1), cat("qnT", 2), cat("qrT", 2)

    mk = _masks("mla")
    wukv = f(mla_w_ukv).reshape(512, 32, 256)
    maps = []
    for core in range(NCORE):
        b, hg = divmod(core, HPB)
        hsl = slice(hg * NH, (hg + 1) * NH)
        wk = np.stack([_chunks(wukv[:, h, :128])[0] for h in range(hg * NH, (hg + 1) * NH)])
        wv = np.stack([_chunks(wukv[:, h, 128:])[0] for h in range(hg * NH, (hg + 1) * NH)])
        maps.append(dict(qT=np.ascontiguousarray(qnT[b][hsl]), qrT=np.ascontiguousarray(qrT[b][hsl]), ckvT=ckvT[b], krT=krT[b],
                         wk=wk, wv=wv, mask=mk))
    res = _run(_prog("att", build_att, "mla"), maps)
    att = np.stack([np.concatenate([res[b * HPB + hg]["oT"] for hg in range(HPB)], axis=0) for b in range(B)])
    att_tok = att.reshape(B, D, SEQ).transpose(0, 2, 1)

    maps = _post_inputs(1, x1, att_tok, ada[1][:, :, :], ln_g[1], ln_b[1], f(mla_w_o)[0], f(ffn_w_up)[1], f(ffn_conv_w)[1],
                        f(ffn_conv_b)[1], f(ffn_w_down)[1], None)
    res = _run(_prog("post", build_post, False), maps)
    out = np.stack([np.concatenate([res[b * HPB + q]["xo"] for q in range(HPB)], axis=1).T for b in range(B)])
    return np.ascontiguousarray(out.astype(np.float32))
```

```python
from contextlib import ExitStack
import numpy as np
import concourse.bass as bass
import concourse.mybir as mybir
from concourse.bass_utils import run_bass_kernel_spmd

F32 = mybir.dt.float32
BF16 = mybir.dt.bfloat16
AF = mybir.ActivationFunctionType
ALU = mybir.AluOpType

D = 4096
B = 2
SEQ = 8192
NCORE = 8
TPC = 2048
TT = 512
KC = D // 128
DFF = 11008
FC = DFF // 128
FQ = 22
QB = [0, 22, 44, 65, 86]
ALPHA = (2.0 * 2) ** 0.25
LN_EPS = 1e-5 / (ALPHA * ALPHA)
RMS_EPS = 1e-6
NEG = -30000.0


class _Sem:
    def __init__(self, handle, step):
        self.h = handle
        self.step = step
        self.count = 0


class _Res:
    __slots__ = ("w", "r")

    def __init__(self):
        self.w = None
        self.r = {}


class Sched:
    def __init__(self, nc, stack):
        self.nc = nc
        self.stack = stack
        self.res = {}
        self.engs = {}
        self.nsem = 0
        for name, e in (("pe", nc.tensor), ("act", nc.scalar), ("dve", nc.vector),
                        ("pool", nc.gpsimd), ("sp", nc.sync)):
            sem = _Sem(self._newsem("e_" + name), 1)
            self.engs[name] = dict(e=e, sem=sem, seen={}, name=name)

    def _newsem(self, name):
        self.nsem += 1
        return self.stack.enter_context(self.nc.semaphore(f"{name}_{self.nsem}"))

    def dma_sem(self, name="d"):
        return _Sem(self._newsem(name), 16)

    def _r(self, key):
        r = self.res.get(key)
        if r is None:
            r = self.res[key] = _Res()
        return r

    def _wait_deps(self, E, reads, writes):
        deps = {}
        for k in reads:
            r = self.res.get(k)
            if r is not None and r.w is not None:
                s, v = r.w
                if deps.get(s, 0) < v:
                    deps[s] = v
        for k in writes:
            r = self.res.get(k)
            if r is not None:
                if r.w is not None:
                    s, v = r.w
                    if deps.get(s, 0) < v:
                        deps[s] = v
                for s, v in r.r.items():
                    if deps.get(s, 0) < v:
                        deps[s] = v
        seen = E["seen"]
        for s, v in deps.items():
            if s is E["sem"] and E["name"] == "pe":
                continue
            if s.step == 16:
                v = s.count
            if seen.get(s, 0) >= v:
                continue
            E["e"].wait_ge(s.h, v)
            seen[s] = v

    def _register(self, ev, reads, writes):
        for k in reads:
            rr = self._r(k).r
            if rr.get(ev[0], 0) < ev[1]:
                rr[ev[0]] = ev[1]
        for k in writes:
            r = self._r(k)
            r.w = ev
            r.r = {}

    def op(self, eng, fn, reads=(), writes=(), mark=True):
        E = self.engs[eng]
        self._wait_deps(E, reads, writes)
        ins = fn(E["e"])
        s = E["sem"]
        if mark:
            ins.then_inc(s.h, 1)
            s.count += 1
            ev = (s, s.count)
        else:
            ev = (s, s.count + 1)
        self._register(ev, reads, writes)
        return ins

    def dma(self, queue, out, in_, reads=(), writes=(), sem=None, **kw):
        E = self.engs[queue]
        self._wait_deps(E, reads, writes)
        ins = E["e"].dma_start(out=out, in_=in_, **kw)
        ins.then_inc(sem.h, 16)
        sem.count += 16
        ev = (sem, sem.count)
        self._register(ev, reads, writes)
        return ev

    def finish(self, queue="sp"):
        E = self.engs[queue]
        allk = list(self.res.keys())
        self._wait_deps(E, allk, allk)


class Ctx:
    def __init__(self):
        self.nc = bass.Bass("TRN2", target_bir_lowering=False)
        self.stack = ExitStack()
        self.S = Sched(self.nc, self.stack)
        self.n = 0

    def sb(self, shape, dt, name="t"):
        self.n += 1
        return self.stack.enter_context(self.nc.sbuf_tensor(f"{name}_{self.n}", list(shape), dt))

    def ps(self, name="ps", shape=(128, 512)):
        self.n += 1
        return self.stack.enter_context(self.nc.psum_tensor(f"{name}_{self.n}", list(shape), F32))

    def din(self, name, shape, dt=F32):
        return self.nc.dram_tensor(name, list(shape), dt, kind="ExternalInput").ap()

    def dout(self, name, shape, dt=F32):
        return self.nc.dram_tensor(name, list(shape), dt, kind="ExternalOutput").ap()

    def dscratch(self, name, shape, dt=F32):
        return self.nc.dram_tensor(name, list(shape), dt, kind="Internal").ap()

    def close(self):
        self.S.finish("sp")
        self.stack.close()
        return self.nc


class Rot:
    def __init__(self, name, bufs, sems=None):
        self.name = name
        self.bufs = bufs
        self.sems = sems
        self.i = -1

    def next(self):
        self.i = (self.i + 1) % len(self.bufs)
        return self.bufs[self.i], (self.name, self.i), (self.sems[self.i] if self.sems else None)


def emit_proj(C, wsrc, wrot, nk, M, rhs_fn, rhs_keys, pst, pkey, W, pcols=None):
    S = C.S
    wb, wkey, wsem = wrot.next()
    S.dma("pool", wb[:, 0:nk, 0:M], wsrc[:, 0:nk, 0:M], writes=[wkey], sem=wsem)
    for k in range(nk):
        S.op("pe", lambda e: e.matmul(pst[0:M, 0:W], lhsT=wb[:, k, 0:M], rhs=rhs_fn(k),
                                      start=(k == 0), stop=(k == nk - 1)),
             reads=[wkey] + rhs_keys(k), writes=[pkey], mark=(k == nk - 1))
    return wb, wkey


def emit_colsum(C, ones, src_fn, src_keys, nchunk, W, psum_t, pkey, sq, tmp_rot):
    S = C.S
    for c in range(nchunk):
        tb, tkey, _ = tmp_rot.next()
        S.op("act", lambda e: e.activation(out=tb[:, 0:W], in_=src_fn(c), func=(AF.Square if sq else AF.Copy)),
             reads=src_keys(c), writes=[tkey])
        S.op("pe", lambda e: e.matmul(psum_t[:, 0:W], lhsT=ones[:, :], rhs=tb[:, 0:W],
                                      start=(c == 0), stop=(c == nchunk - 1)),
             reads=[tkey, "ones"], writes=[pkey])


def emit_ln(C, K, xt, g, cs, W, gcol, bcol, sccol, shcol, hb, hbkey):
    S = C.S
    vec = K["vec"]
    emit_colsum(C, K["ones"], lambda c: xt[:, c, cs], lambda c: [("xt", c, g)], KC, W, K["psL"][0], "psL0", False, K["tmpb"])
    emit_colsum(C, K["ones"], lambda c: xt[:, c, cs], lambda c: [("xt", c, g)], KC, W, K["psL"][1], "psL1", True, K["tmpb"])
    mean, msq, rstd = K["st"][0], K["st"][1], K["st"][2]
    S.op("dve", lambda e: e.tensor_scalar(out=mean[:, 0:W], in0=K["psL"][0][:, 0:W], scalar1=1.0 / D, scalar2=None, op0=ALU.mult),
         reads=["psL0"], writes=["mean"])
    S.op("dve", lambda e: e.tensor_tensor(out=msq[:, 0:W], in0=mean[:, 0:W], in1=mean[:, 0:W], op=ALU.mult),
         reads=["mean"], writes=["msq"])
    S.op("dve", lambda e: e.scalar_tensor_tensor(out=msq[:, 0:W], in0=K["psL"][1][:, 0:W], scalar=1.0 / D, in1=msq[:, 0:W],
                                                 op0=ALU.mult, op1=ALU.subtract),
         reads=["psL1", "msq"], writes=["msq"])
    S.op("dve", lambda e: e.tensor_scalar(out=msq[:, 0:W], in0=msq[:, 0:W], scalar1=LN_EPS, scalar2=None, op0=ALU.add),
         reads=["msq"], writes=["msq"])
    S.op("act", lambda e: e.activation(out=msq[:, 0:W], in_=msq[:, 0:W], func=AF.Sqrt), reads=["msq"], writes=["msq"])
    S.op("dve", lambda e: e.reciprocal(out=rstd[:, 0:W], in_=msq[:, 0:W]), reads=["msq"], writes=["rstd"])
    for c in range(KC):
        S.op("dve", lambda e: e.tensor_tensor(out=xt[:, c, cs], in0=xt[:, c, cs], in1=mean[:, 0:W], op=ALU.subtract),
             reads=[("xt", c, g), "mean"], writes=[("xt", c, g)])
        S.op("dve", lambda e: e.tensor_tensor(out=xt[:, c, cs], in0=xt[:, c, cs], in1=rstd[:, 0:W], op=ALU.mult),
             reads=[("xt", c, g), "rstd"], writes=[("xt", c, g)])
        S.op("act", lambda e: e.activation(out=xt[:, c, cs], in_=xt[:, c, cs], func=AF.Identity,
                                           bias=vec[:, bcol + c:bcol + c + 1], scale=vec[:, gcol + c:gcol + c + 1]),
             reads=[("xt", c, g), "vec"], writes=[("xt", c, g)])
        if hb is not None:
            S.op("act", lambda e: e.activation(out=hb[:, c, cs], in_=xt[:, c, cs], func=AF.Identity,
                                               bias=vec[:, shcol + c:shcol + c + 1], scale=vec[:, sccol + c:sccol + c + 1]),
                 reads=[("xt", c, g), "vec"], writes=[(hbkey, c, g)])


ADA_N = 3 * D // NCORE


def build_ada():
    C = Ctx()
    nc, S = C.nc, C.S
    cT = C.din("cT", [128, KC, B])
    w = C.din("w", [4, D, ADA_N])
    bias = C.din("bias", [B, 4 * ADA_N])
    o = C.dout("o", [B, 4 * ADA_N])
    ct = C.sb([128, KC, B], F32, "ct")
    ca = C.sb([128, KC, B], F32, "ca")
    bt = C.sb([B, 4 * ADA_N], F32, "bt")
    ot = C.sb([B, 4 * ADA_N], F32, "ot")
    wts = [C.sb([128, KC, 512], F32, "w") for _ in range(2)]
    wrot = Rot("w", wts, [S.dma_sem("w") for _ in range(2)])
    pss = Rot("ps", [C.ps() for _ in range(2)])
    ds = S.dma_sem("m")
    S.dma("sp", ct[:], cT[:, :, :], writes=["ct"], sem=ds)
    S.dma("sp", bt[:], bias[:, :], writes=["bt"], sem=ds)
    S.op("act", lambda e: e.activation(out=ca[:], in_=ct[:], func=AF.Silu), reads=["ct"], writes=["ca"])
    for m in range(4):
        wv = w[m].rearrange("(kc p) n -> p kc n", p=128)
        for t in range(ADA_N // 512):
            wb, wkey, wsem = wrot.next()
            S.dma("sp", wb[:], wv[:, :, t * 512:(t + 1) * 512], writes=[wkey], sem=wsem)
            pt, pkey, _ = pss.next()
            for k in range(KC):
                S.op("pe", lambda e: e.matmul(pt[0:B, :], lhsT=ca[:, k, :], rhs=wb[:, k, :], start=(k == 0), stop=(k == KC - 1)),
                     reads=[wkey, "ca"], writes=[pkey], mark=(k == KC - 1))
            off = m * ADA_N + t * 512
            S.op("dve", lambda e: e.tensor_tensor(out=ot[:, off:off + 512], in0=pt[0:B, :], in1=bt[:, off:off + 512], op=ALU.add),
                 reads=[pkey, "bt"], writes=["ot"])
    S.dma("sp", o[:, :], ot[:], reads=["ot"], sem=ds)
    return C.close()


PRE_VEC = dict(sc=0, sh=32, nbf=64, n=65)


def build_pre0():
    C = Ctx()
    nc, S = C.nc, C.S
    xT = C.din("xT", [D, TPC])
    vecd = C.din("vec", [128, PRE_VEC["n"]])
    wq = C.din("wq", [96, 128, KC, 128])
    wf = C.din("wf", [128, KC, 32])
    qkvT = C.dout("qkvT", [96, 128, TPC])
    logfT = C.dout("logfT", [32, TPC])
    xv = xT.rearrange("(c p) t -> p c t", p=128)
    vec = C.sb([128, PRE_VEC["n"]], F32, "vec")
    xt = C.sb([128, KC, TT], F32, "xt")
    hb = C.sb([128, KC, TT], BF16, "hb")
    wfb = C.sb([128, KC, 32], BF16, "wfb")
    wrot = Rot("w", [C.sb([128, KC, 128], BF16, "w") for _ in range(3)], [S.dma_sem("w") for _ in range(3)])
    pss = Rot("ps", [C.ps() for _ in range(4)])
    obr = Rot("ob", [C.sb([128, TT], F32, "ob") for _ in range(4)], [S.dma_sem("o") for _ in range(4)])
    fb = [C.sb([32, TT], F32, "fb") for _ in range(2)]
    ds = S.dma_sem("m")
    xs = S.dma_sem("x")
    S.dma("sp", vec[:], vecd[:, :], writes=["vec"], sem=ds)
    S.dma("pool", wfb[:], wf[:, :, :], writes=["wfb"], sem=ds)
    sc1 = C.sb([128, KC], F32, "sc1")
    S.op("dve", lambda e: e.tensor_scalar(out=sc1[:], in0=vec[:, 0:32], scalar1=1.0, scalar2=None, op0=ALU.add),
         reads=["vec"], writes=["sc1"])
    qscale = 128.0 ** -0.5
    for t in range(TPC // TT):
        cols = slice(t * TT, (t + 1) * TT)
        S.dma("sp", xt[:], xv[:, :, cols], writes=[("xt", c, 0) for c in range(KC)], sem=xs)
        for c in range(KC):
            S.op("act", lambda e: e.activation(out=hb[:, c, :], in_=xt[:, c, :], func=AF.Identity,
                                               bias=vec[:, 32 + c:33 + c], scale=sc1[:, c:c + 1]),
                 reads=[("xt", c, 0), "vec", "sc1"], writes=[("hb", c)])
        for oc in range(96):
            pt, pkey, _ = pss.next()
            emit_proj(C, wq[oc], wrot, KC, 128, lambda k: hb[:, k, :], lambda k: [("hb", k)], pt, pkey, TT)
            ob, okey, osem = obr.next()
            sc = qscale if oc < 32 else 1.0
            if oc % 2 == 0:
                S.op("act", lambda e: e.activation(out=ob[:], in_=pt[:], func=AF.Copy, scale=sc), reads=[pkey], writes=[okey])
            else:
                S.op("dve", lambda e: e.tensor_scalar(out=ob[:], in0=pt[:], scalar1=sc, scalar2=None, op0=ALU.mult),
                     reads=[pkey], writes=[okey])
            S.dma("sp", qkvT[oc][:, cols], ob[:], reads=[okey], sem=osem)
        pt, pkey, _ = pss.next()
        for k in range(KC):
            S.op("pe", lambda e: e.matmul(pt[0:32, :], lhsT=wfb[:, k, :], rhs=hb[:, k, :], start=(k == 0), stop=(k == KC - 1)),
                 reads=["wfb", ("hb", k)], writes=[pkey], mark=(k == KC - 1))
        S.op("act", lambda e: e.activation(out=fb[0][:], in_=pt[0:32, :], func=AF.Exp, bias=vec[0:32, 64:65], scale=-1.0),
             reads=[pkey, "vec"], writes=["fb0"])
        S.op("act", lambda e: e.activation(out=fb[0][:], in_=fb[0][:], func=AF.Ln, bias=1.0, scale=1.0),
             reads=["fb0"], writes=["fb0"])
        S.op("dve", lambda e: e.tensor_scalar(out=fb[1][:], in0=fb[0][:], scalar1=-1.0, scalar2=None, op0=ALU.mult),
             reads=["fb0"], writes=["fb1"])
        S.dma("sp", logfT[:, cols], fb[1][:], reads=["fb1"], sem=ds)
    return C.close()


NH = 8
QT = 512
NQT = SEQ // QT
NKT = SEQ // 128


def build_att(kind):
    fox = (kind == "fox")
    LA = 3
    C = Ctx()
    nc, S = C.nc, C.S
    qT = C.din("qT", [NH, 128, SEQ])
    maskd = C.din("mask", [4, 128, QT])
    oT = C.dout("oT", [NH, 128, SEQ])
    if fox:
        kT = C.din("kT", [NH, 128, SEQ])
        vtok = C.din("vtok", [NH, SEQ, 128])
        logf = C.din("logf", [NH, SEQ])
        cumd = C.dscratch("cumd", [NH, SEQ])
    else:
        qrT = C.din("qrT", [NH, 64, SEQ])
        ckvT = C.din("ckvT", [512, SEQ])
        krT = C.din("krT", [64, SEQ])
        wk = C.din("wk", [NH, 128, 4, 128])
        wv = C.din("wv", [NH, 128, 4, 128])
    ones = C.sb([128, 128], BF16, "ones")
    S.op("pool", lambda e: e.memset(ones[:], 1.0), writes=["ones"])
    mask = C.sb([128, 4, QT], F32, "mask")
    ds = S.dma_sem("m")
    for o in range(4):
        S.dma("sp", mask[:, o, :], maskd[o], writes=["mask"], sem=ds)
    NBQ = 2 if fox else 1
    Qb = [C.sb([128, SEQ], BF16, "Qb") for _ in range(NBQ)]
    Kb = [C.sb([128, SEQ], BF16, "Kb") for _ in range(2)]
    Vb = [C.sb([128, NKT, 128], BF16, "Vb") for _ in range(2)]
    hsq = [S.dma_sem("hq") for _ in range(2)]
    hsk = [S.dma_sem("hk") for _ in range(2)]
    hsv = [S.dma_sem("hv") for _ in range(2)]
    psS = Rot("psS", [C.ps() for _ in range(4)])
    psO = Rot("psO", [C.ps() for _ in range(2)])
    psLr = Rot("psLs", [C.ps() for _ in range(2)])
    ptr = Rot("pt", [C.sb([128, QT], BF16, "pt") for _ in range(4)])
    tfr = Rot("tf", [C.sb([128, QT], F32, "tf") for _ in range(3)])
    obr = Rot("ob", [C.sb([128, QT], F32, "ob") for _ in range(2)], [S.dma_sem("o") for _ in range(2)])
    rinv = C.sb([128, QT], F32, "rinv")
    if fox:
        SC = 2048
        lf = C.sb([NH, SC], F32, "lf")
        onesf = C.sb([NH, SC], F32, "onesf")
        cum = [C.sb([NH, SC], F32, "cum") for _ in range(2)]
        S.op("pool", lambda e: e.memset(onesf[:], 1.0), writes=["onesf"])
        for sg_ in range(SEQ // SC):
            sl = slice(sg_ * SC, (sg_ + 1) * SC)
            cb_, cprev = cum[sg_ % 2], cum[(sg_ + 1) % 2]
            S.dma("sp", lf[:], logf[:, sl], writes=["lf"], sem=ds)
            init = 0.0 if sg_ == 0 else cprev[:, SC - 1:SC]
            S.op("dve", lambda e: e.tensor_tensor_scan(out=cb_[:], data0=onesf[:], data1=lf[:], initial=init,
                                                       op0=ALU.mult, op1=ALU.add),
                 reads=["lf", "onesf", ("cum", (sg_ + 1) % 2)], writes=[("cum", sg_ % 2)])
            S.dma("sp", cumd[:, sl], cb_[:], reads=[("cum", sg_ % 2)], writes=["cumd"], sem=ds)
        cqb = C.sb([128, SEQ], F32, "cqb")
        nck = [C.sb([128, NKT], F32, "nck") for _ in range(2)]
        cqm = C.sb([128, 4, QT], F32, "cqm")
        cqs = S.dma_sem("cq")
    else:
        QRb = C.sb([64, SEQ], BF16, "QRb")
        KRb = C.sb([64, SEQ], BF16, "KRb")
        Cb = C.sb([128, 4, SEQ], BF16, "Cb")
        wkb = [C.sb([128, 4, 128], BF16, "wkb") for _ in range(2)]
        wvb = [C.sb([128, 4, 128], BF16, "wvb") for _ in range(2)]
        S.dma("pool", KRb[:], krT[:, :], writes=["KRb"], sem=ds)
        for cc in range(4):
            S.dma("pool", Cb[:, cc, :], ckvT[cc * 128:(cc + 1) * 128, :], writes=["Cb"], sem=ds)

    def prologue(h):
        hb_ = h % 2
        if fox:
            S.dma("pool", Qb[hb_][:], qT[h], writes=[("Qb", hb_)], sem=hsq[hb_])
            S.dma("pool", Kb[hb_][:], kT[h], writes=[("Kb", hb_)], sem=hsk[hb_])
            S.dma("pool", Vb[hb_][:], vtok[h].rearrange("(t p) d -> p t d", p=128), writes=[("Vb", hb_)], sem=hsv[hb_])
            with nc.allow_non_contiguous_dma(reason="cum column layout"):
                S.dma("sp", nck[hb_][:], cumd[h].rearrange("(t p) -> p t", p=128), reads=["cumd"], writes=[("nck", hb_)], sem=hsk[hb_])
            S.op("dve", lambda e: e.tensor_scalar(out=nck[hb_][:], in0=nck[hb_][:], scalar1=-1.0, scalar2=None, op0=ALU.mult),
                 reads=[("nck", hb_)], writes=[("nck", hb_)])
        else:
            S.dma("pool", wkb[hb_][:], wk[h], writes=[("wkb", hb_)], sem=hsk[hb_])
            S.dma("pool", wvb[hb_][:], wv[h], writes=[("wvb", hb_)], sem=hsv[hb_])
            for t in range(NQT):
                pt_, pkey, _ = psS.next()
                for cc in range(4):
                    S.op("pe", lambda e: e.matmul(pt_[:, :], lhsT=wkb[hb_][:, cc, :], rhs=Cb[:, cc, t * QT:(t + 1) * QT],
                                                  start=(cc == 0), stop=(cc == 3)),
                         reads=[("wkb", hb_), "Cb"], writes=[pkey], mark=(cc == 3))
                S.op("act", lambda e: e.activation(out=Kb[hb_][:, t * QT:(t + 1) * QT], in_=pt_[:, :], func=AF.Copy),
                     reads=[pkey], writes=[("Kb", hb_)])
            for kt4 in range(NKT // 4):
                pt_, pkey, _ = psS.next()
                for j in range(4):
                    kt = kt4 * 4 + j
                    for cc in range(4):
                        S.op("pe", lambda e: e.matmul(pt_[:, j * 128:(j + 1) * 128], lhsT=Cb[:, cc, kt * 128:(kt + 1) * 128],
                                                      rhs=wvb[hb_][:, cc, :], start=(cc == 0), stop=(cc == 3)),
                             reads=[("wvb", hb_), "Cb"], writes=[pkey], mark=(cc == 3 and j == 3))
                S.op("dve", lambda e: e.tensor_copy(out=Vb[hb_][:, kt4 * 4:(kt4 + 1) * 4, :],
                                                    in_=pt_[:, :].rearrange("p (j d) -> p j d", j=4)),
                     reads=[pkey], writes=[("Vb", hb_)])

    prologue(0)
    for h in range(NH):
        hb_ = h % 2
        qb_ = h % NBQ
        Q, Kt, V = Qb[qb_], Kb[hb_], Vb[hb_]
        qkey, kkey, vkey = ("Qb", qb_), ("Kb", hb_), ("Vb", hb_)
        if fox:
            S.dma("sp", cqb[:], cumd[h:h + 1, :].partition_broadcast(128), reads=["cumd"], writes=["cqb"], sem=cqs)
        else:
            S.dma("pool", Q[:], qT[h], writes=[qkey], sem=hsq[0])
            S.dma("pool", QRb[:], qrT[h], writes=["QRb"], sem=hsq[1])
        if h + 1 < NH:
            prologue(h + 1)
        blocks = [(qt, kt) for qt in range(NQT) for kt in range(4 * qt + 4)]
        st = {}
        qst = {}

        def stage_a(i):
            qt, kt = blocks[i]
            qs = slice(qt * QT, (qt + 1) * QT)
            ks = slice(kt * 128, (kt + 1) * 128)
            diag = kt - 4 * qt
            if kt == 0:
                if fox:
                    for o in range(4):
                        S.op("pool", lambda e: e.tensor_tensor(out=cqm[:, o, :], in0=cqb[:, qs], in1=mask[:, o, :], op=ALU.add),
                             reads=["cqb", "mask"], writes=[("cqm", o)])
                qst[qt] = (psO.next(), psLr.next())
            pS, pskey, _ = psS.next()
            if fox:
                S.op("pe", lambda e: e.matmul(pS[:, :], lhsT=Kt[:, ks], rhs=Q[:, qs], start=True, stop=True),
                     reads=[kkey, qkey], writes=[pskey])
            else:
                S.op("pe", lambda e: e.matmul(pS[:, :], lhsT=Kt[:, ks], rhs=Q[:, qs], start=True, stop=False),
                     reads=[kkey, qkey], writes=[pskey], mark=False)
                S.op("pe", lambda e: e.matmul(pS[:, :], lhsT=KRb[:, ks], rhs=QRb[:, qs], start=False, stop=True),
                     reads=["KRb", "QRb"], writes=[pskey])
            pt_, ptkey, _ = ptr.next()
            if fox:
                tf, tfkey, _ = tfr.next()
                if diag >= 0:
                    S.op("dve", lambda e: e.tensor_tensor(out=tf[:], in0=pS[:, :], in1=cqm[:, diag, :], op=ALU.add),
                         reads=[pskey, ("cqm", diag)], writes=[tfkey])
                else:
                    S.op("dve", lambda e: e.tensor_tensor(out=tf[:], in0=pS[:, :], in1=cqb[:, qs], op=ALU.add),
                         reads=[pskey, "cqb"], writes=[tfkey])
                S.op("act", lambda e: e.activation(out=pt_[:], in_=tf[:], func=AF.Exp, bias=nck[hb_][:, kt:kt + 1], scale=1.0),
                     reads=[tfkey, ("nck", hb_)], writes=[ptkey])
            else:
                if diag >= 0:
                    tf, tfkey, _ = tfr.next()
                    S.op("dve", lambda e: e.tensor_tensor(out=tf[:], in0=pS[:, :], in1=mask[:, diag, :], op=ALU.add),
                         reads=[pskey, "mask"], writes=[tfkey])
                    S.op("act", lambda e: e.activation(out=pt_[:], in_=tf[:], func=AF.Exp), reads=[tfkey], writes=[ptkey])
                else:
                    S.op("act", lambda e: e.activation(out=pt_[:], in_=pS[:, :], func=AF.Exp), reads=[pskey], writes=[ptkey])
            st[i] = (pt_, ptkey)

        def stage_b(i):
            qt, kt = blocks[i]
            qs = slice(qt * QT, (qt + 1) * QT)
            nkt = 4 * qt + 4
            (po, pokey, _), (pl, plkey, _) = qst[qt]
            pt_, ptkey = st.pop(i)
            last = (kt == nkt - 1)
            S.op("pe", lambda e: e.matmul(po[:, :], lhsT=V[:, kt, :], rhs=pt_[:], start=(kt == 0), stop=last),
                 reads=[vkey, ptkey], writes=[pokey], mark=last)
            S.op("pe", lambda e: e.matmul(pl[:, :], lhsT=ones[:, :], rhs=pt_[:], start=(kt == 0), stop=last),
                 reads=["ones", ptkey], writes=[plkey], mark=True)
            if last:
                S.op("dve", lambda e: e.reciprocal(out=rinv[:], in_=pl[:, :]), reads=[plkey], writes=["rinv"])
                ob, okey, osem = obr.next()
                S.op("dve", lambda e: e.tensor_tensor(out=ob[:], in0=po[:, :], in1=rinv[:], op=ALU.mult),
                     reads=[pokey, "rinv"], writes=[okey])
                S.dma("sp", oT[h][:, qs], ob[:], reads=[okey], sem=osem)
                del qst[qt]

        nb = len(blocks)
        for i in range(nb + LA):
            if i < nb:
                stage_a(i)
            if i - LA >= 0:
                stage_b(i - LA)
    return C.close()


def post_vec_layout(extras):
    names = [("gate1", 32), ("lng0", 32), ("lnb0", 32), ("sc2", 32), ("sh2", 32), ("gate2", 32),
             ("lng1", 32), ("lnb1", 32), ("cw0", 172), ("cw1", 172), ("cw2", 172), ("cb", 172), ("hflag", 1)]
    if extras:
        names += [("sc3", 32), ("sh3", 32), ("kvn", 4), ("qn", 8)]
    off = {}
    o = 0
    for n, w in names:
        off[n] = o
        o += w
    off["n"] = o
    return off


def build_post(extras):
    L = post_vec_layout(extras)
    C = Ctx()
    nc, S = C.nc, C.S
    NCOL = TPC + 2
    xT = C.din("xT", [D, NCOL])
    aT = C.din("aT", [D, NCOL])
    vecd = C.din("vec", [128, L["n"]])
    wo = C.din("wo", [KC, 128, KC, 128])
    wup = C.din("wup", [2 * FC, 128, KC, 128])
    wdn = C.din("wdn", [4, KC, 128, FQ, 128])
    xo = C.dout("xo", [D, TPC])
    xv = xT.rearrange("(c p) t -> p c t", p=128)
    av = aT.rearrange("(c p) t -> p c t", p=128)
    xov = xo.rearrange("(c p) t -> p c t", p=128)
    if extras:
        ropeC = C.din("ropeC", [64, TPC])
        ropeS = C.din("ropeS", [64, TPC])
        wdkv = C.din("wdkv", [6, 128, KC, 128])
        wdq = C.din("wdq", [8, 128, KC, 128])
        wuq = C.din("wuq", [32, 3, 128, 8, 128])
        ckvT = C.dout("ckvT", [512, TPC])
        krT = C.dout("krT", [64, TPC])
        qnT = C.dout("qnT", [32, 128, TPC])
        qrT = C.dout("qrT", [32, 64, TPC])

    NV = L["n"]
    vec2 = C.sb([128, NV + 128], F32, "vec2")
    vec = vec2
    dv = vec2[:, NV:NV + 128]
    ones = C.sb([128, 128], BF16, "ones")
    S.op("pool", lambda e: e.memset(ones[:], 1.0), writes=["ones"])
    ds = S.dma_sem("m")
    S.dma("sp", vec2[:, 0:NV], vecd[:, :], writes=["vec"], sem=ds)
    S.op("dve", lambda e: e.tensor_scalar(out=dv[:, 0:32], in0=vec[:, L["gate1"]:L["gate1"] + 32], scalar1=1.0, scalar2=1.0 / ALPHA,
                                          op0=ALU.add, op1=ALU.mult), reads=["vec"], writes=["vec"])
    S.op("dve", lambda e: e.tensor_scalar(out=dv[:, 32:64], in0=vec[:, L["gate2"]:L["gate2"] + 32], scalar1=1.0, scalar2=1.0 / ALPHA,
                                          op0=ALU.add, op1=ALU.mult), reads=["vec"], writes=["vec"])
    S.op("dve", lambda e: e.tensor_scalar(out=dv[:, 64:96], in0=vec[:, L["sc2"]:L["sc2"] + 32], scalar1=1.0, scalar2=None,
                                          op0=ALU.add), reads=["vec"], writes=["vec"])
    if extras:
        S.op("dve", lambda e: e.tensor_scalar(out=dv[:, 96:128], in0=vec[:, L["sc3"]:L["sc3"] + 32], scalar1=1.0, scalar2=None,
                                              op0=ALU.add), reads=["vec"], writes=["vec"])
    G1, G2, SC2, SC3 = NV, NV + 32, NV + 64, NV + 96

    XW = TT + 2
    xt = C.sb([128, KC, XW], F32, "xt")
    hb = C.sb([128, KC, XW], BF16, "hb")
    act = C.sb([128, FQ, TT], BF16, "act")
    uh = C.sb([128, 2 * FC, 2], F32, "uh")
    wrot = Rot("w", [C.sb([128, KC, 128], BF16, "w") for _ in range(3)], [S.dma_sem("w") for _ in range(3)])
    psA = Rot("psA", [C.ps() for _ in range(2)])
    psU = Rot("psU", [C.ps() for _ in range(4)])
    psL = [C.ps(), C.ps()]
    K = dict(vec=vec2, ones=ones, psL=psL,
             tmpb=Rot("tmpb", [C.sb([128, TT], BF16, "tmpb") for _ in range(2)]),
             st=[C.sb([128, TT], F32, "st") for _ in range(3)])
    ubr = Rot("ub", [C.sb([128, XW], F32, "ub") for _ in range(3)], [S.dma_sem("ub") for _ in range(3)])
    cvr = Rot("cv", [C.sb([128, TT], F32, "cv") for _ in range(4)], [S.dma_sem("cv") for _ in range(4)])
    sgr = Rot("sg", [C.sb([128, TT], F32, "sg") for _ in range(2)])
    xs = S.dma_sem("x")
    os_ = S.dma_sem("o")
    S.op("pool", lambda e: e.memset(uh[:], 0.0), writes=["uh"])

    for t in range(TPC // TT):
        groups = [(1, slice(2, XW), TT)]
        if t == 0:
            groups = [(0, slice(0, 2), 2)] + groups
        c0 = 2 + t * TT
        lo = 0 if t == 0 else c0
        so = 0 if t == 0 else 2
        gk = [0, 1] if t == 0 else [1]
        S.dma("sp", xt[:, :, so:XW], xv[:, :, lo:c0 + TT], writes=[("xt", c, g) for c in range(KC) for g in gk], sem=xs)
        S.dma("pool", hb[:, :, so:XW], av[:, :, lo:c0 + TT], writes=[("hb", c, g) for c in range(KC) for g in gk], sem=xs)
        for dc in range(KC):
            wb = wkey = None
            for (g, cs, W) in groups:
                pt, pkey, _ = psA.next()
                if wb is None:
                    wb, wkey = emit_proj(C, wo[dc], wrot, KC, 128, lambda k: hb[:, k, cs], lambda k: [("hb", k, g)], pt, pkey, W)
                else:
                    for k in range(KC):
                        S.op("pe", lambda e: e.matmul(pt[:, 0:W], lhsT=wb[:, k, :], rhs=hb[:, k, cs], start=(k == 0), stop=(k == KC - 1)),
                             reads=[wkey, ("hb", k, g)], writes=[pkey], mark=(k == KC - 1))
                S.op("dve", lambda e: e.scalar_tensor_tensor(out=xt[:, dc, cs], in0=pt[:, 0:W], scalar=vec2[:, G1 + dc:G1 + dc + 1],
                                                             in1=xt[:, dc, cs], op0=ALU.mult, op1=ALU.add),
                     reads=[pkey, ("xt", dc, g), "vec"], writes=[("xt", dc, g)])
        for (g, cs, W) in groups:
            emit_ln(C, K, xt, g, cs, W, L["lng0"], L["lnb0"], SC2, L["sh2"], hb, "hb")
        for half in range(4):
            nq = QB[half + 1] - QB[half]
            for j in range(nq):
                jj = QB[half] + j
                cvs = []
                for which, ci in (("a", jj), ("g", FC + jj)):
                    wb = wkey = None
                    if t == 0:
                        pt, pkey, _ = psA.next()
                        wb, wkey = emit_proj(C, wup[ci], wrot, KC, 128, lambda k: hb[:, k, 0:2], lambda k: [("hb", k, 0)], pt, pkey, 2)
                        S.op("dve", lambda e: e.tensor_scalar(out=uh[:, ci, :], in0=pt[:, 0:2], scalar1=vec2[:, L["hflag"]:L["hflag"] + 1],
                                                              scalar2=None, op0=ALU.mult),
                             reads=[pkey, "vec"], writes=[("uh", ci)])
                    pt, pkey, _ = psU.next()
                    if wb is None:
                        wb, wkey = emit_proj(C, wup[ci], wrot, KC, 128, lambda k: hb[:, k, 2:XW], lambda k: [("hb", k, 1)], pt, pkey, TT)
                    else:
                        for k in range(KC):
                            S.op("pe", lambda e: e.matmul(pt[:, :], lhsT=wb[:, k, :], rhs=hb[:, k, 2:XW], start=(k == 0), stop=(k == KC - 1)),
                                 reads=[wkey, ("hb", k, 1)], writes=[pkey], mark=(k == KC - 1))
                    ub, ukey, _ = ubr.next()
                    S.op("act", lambda e: e.activation(out=ub[:, 2:XW], in_=pt[:, :], func=AF.Copy), reads=[pkey], writes=[ukey])
                    S.op("act", lambda e: e.activation(out=ub[:, 0:2], in_=uh[:, ci, :], func=AF.Copy), reads=[("uh", ci)], writes=[ukey])
                    S.op("act", lambda e: e.activation(out=uh[:, ci, :], in_=ub[:, TT:XW], func=AF.Copy), reads=[ukey], writes=[("uh", ci)])
                    cv, ckey, _ = cvr.next()
                    S.op("dve", lambda e: e.tensor_scalar(out=cv[:], in0=ub[:, 2:XW], scalar1=vec2[:, L["cw2"] + ci:L["cw2"] + ci + 1],
                                                          scalar2=vec2[:, L["cb"] + ci:L["cb"] + ci + 1], op0=ALU.mult, op1=ALU.add),
                         reads=[ukey, "vec"], writes=[ckey])
                    S.op("dve", lambda e: e.scalar_tensor_tensor(out=cv[:], in0=ub[:, 1:XW - 1], scalar=vec2[:, L["cw1"] + ci:L["cw1"] + ci + 1],
                                                                 in1=cv[:], op0=ALU.mult, op1=ALU.add),
                         reads=[ukey, ckey, "vec"], writes=[ckey])
                    S.op("dve", lambda e: e.scalar_tensor_tensor(out=cv[:], in0=ub[:, 0:TT], scalar=vec2[:, L["cw0"] + ci:L["cw0"] + ci + 1],
                                                                 in1=cv[:], op0=ALU.mult, op1=ALU.add),
                         reads=[ukey, ckey, "vec"], writes=[ckey])
                    cvs.append((cv, ckey))
                sg, sgkey, _ = sgr.next()
                S.op("act", lambda e: e.activation(out=sg[:], in_=cvs[1][0][:], func=AF.Silu), reads=[cvs[1][1]], writes=[sgkey])
                S.op("dve", lambda e: e.tensor_tensor(out=act[:, j, :], in0=cvs[0][0][:], in1=sg[:], op=ALU.mult),
                     reads=[cvs[0][1], sgkey], writes=[("act", j)])
            for dc in range(KC):
                pt, pkey, _ = psA.next()
                emit_proj(C, wdn[half, dc], wrot, nq, 128, lambda k: act[:, k, :], lambda k: [("act", k)], pt, pkey, TT)
                S.op("dve", lambda e: e.scalar_tensor_tensor(out=xt[:, dc, 2:XW], in0=pt[:, :], scalar=vec2[:, G2 + dc:G2 + dc + 1],
                                                             in1=xt[:, dc, 2:XW], op0=ALU.mult, op1=ALU.add),
                     reads=[pkey, ("xt", dc, 1), "vec"], writes=[("xt", dc, 1)])
        cs = slice(2, XW)
        emit_ln(C, K, xt, 1, cs, TT, L["lng1"], L["lnb1"], 0, 0, None, None)
        ocols = slice(t * TT, (t + 1) * TT)
        S.dma("sp", xov[:, :, ocols], xt[:, :, 2:XW], reads=[("xt", c, 1) for c in range(KC)], sem=os_)
        if extras:
            emit_mla_extras(C, K, L, dict(xt=xt, hb=hb, act=act, vec2=vec2, ones=ones, wrot=wrot, psA=psA, psU=psU,
                                          ocols=ocols, XW=XW, ckvT=ckvT, krT=krT, qnT=qnT, qrT=qrT, ropeC=ropeC, ropeS=ropeS,
                                          wdkv=wdkv, wdq=wdq, wuq=wuq, SC3=SC3, cvr=cvr, ubr=ubr))
    return C.close()


def emit_mla_extras(C, K, L, V):
    S = C.S
    xt, hb, act, vec2, ones, wrot, psA, psU = (V[k] for k in ("xt", "hb", "act", "vec2", "ones", "wrot", "psA", "psU"))
    ocols, XW, SC3, cvr, ubr = V["ocols"], V["XW"], V["SC3"], V["cvr"], V["ubr"]
    ckvT, krT, qnT, qrT, ropeC, ropeS, wdkv, wdq, wuq = (V[k] for k in ("ckvT", "krT", "qnT", "qrT", "ropeC", "ropeS", "wdkv", "wdq", "wuq"))
    E = K.get("_extras")
    if E is None:
        E = K["_extras"] = dict(qdn=C.sb([128, 8, TT], BF16, "qdn"), rc=C.sb([64, TT], F32, "rc"), rs=C.sb([64, TT], F32, "rs"),
                                ds=S.dma_sem("em"))
    qdn, rc, rs = E["qdn"], E["rc"], E["rs"]
    latv = act[:, :, :].rearrange("p a b -> p (a b)").bitcast(F32)
    lat = lambda c: latv[:, c * TT:(c + 1) * TT]
    latk = lambda c: [("act", 2 * c), ("act", 2 * c + 1)]
    S.dma("sp", rc[:], ropeC[:, ocols], writes=["rc"], sem=E["ds"])
    S.dma("sp", rs[:], ropeS[:, ocols], writes=["rs"], sem=E["ds"])
    for c in range(KC):
        S.op("act", lambda e: e.activation(out=hb[:, c, 2:XW], in_=xt[:, c, 2:XW], func=AF.Copy),
             reads=[("xt", c, 1)], writes=[("hb", c, 1)])
    rstd = K["st"][2]

    def rms(nch, gcol, out_fn, out_keys, after=None):
        emit_colsum(C, ones, lat, latk, nch, TT, K["psL"][1], "psL1", True, K["tmpb"])
        S.op("dve", lambda e: e.tensor_scalar(out=rstd[:], in0=K["psL"][1][:, :], scalar1=1.0 / (128 * nch), scalar2=RMS_EPS,
                                              op0=ALU.mult, op1=ALU.add), reads=["psL1"], writes=["rstd"])
        S.op("act", lambda e: e.activation(out=rstd[:], in_=rstd[:], func=AF.Sqrt), reads=["rstd"], writes=["rstd"])
        S.op("dve", lambda e: e.reciprocal(out=rstd[:], in_=rstd[:]), reads=["rstd"], writes=["rstd"])
        for c in range(nch):
            o, okeys = out_fn(c)
            S.op("dve", lambda e: e.scalar_tensor_tensor(out=o, in0=lat(c), scalar=vec2[:, gcol + c:gcol + c + 1],
                                                         in1=rstd[:], op0=ALU.mult, op1=ALU.mult),
                 reads=latk(c) + ["rstd", "vec"], writes=okeys)
            if after is not None:
                after(c, okeys)

    def rope_pair(wsrc_r, wsrc_rot, nk, rhs_fn, rhs_keys, scale, dst):
        outs = []
        for wsrc in (wsrc_r, wsrc_rot):
            pt, pkey, _ = psU.next()
            emit_proj(C, wsrc, wrot, nk, 64, rhs_fn, rhs_keys, pt, pkey, TT)
            outs.append((pt, pkey))
        a, akey, asem = cvr.next()
        b, bkey, _ = cvr.next()
        S.op("dve", lambda e: e.scalar_tensor_tensor(out=a[0:64, :], in0=outs[0][0][0:64, :], scalar=scale, in1=rc[:], op0=ALU.mult, op1=ALU.mult),
             reads=[outs[0][1], "rc"], writes=[akey])
        S.op("dve", lambda e: e.scalar_tensor_tensor(out=b[0:64, :], in0=outs[1][0][0:64, :], scalar=scale, in1=rs[:], op0=ALU.mult, op1=ALU.mult),
             reads=[outs[1][1], "rs"], writes=[bkey])
        S.op("dve", lambda e: e.tensor_tensor(out=a[0:64, :], in0=a[0:64, :], in1=b[0:64, :], op=ALU.add),
             reads=[akey, bkey], writes=[akey])
        S.dma("sp", dst, a[0:64, :], reads=[akey], sem=asem)

    for cc in range(4):
        pt, pkey, _ = psA.next()
        emit_proj(C, wdkv[cc], wrot, KC, 128, lambda k: hb[:, k, 2:XW], lambda k: [("hb", k, 1)], pt, pkey, TT)
        S.op("act", lambda e: e.activation(out=lat(cc), in_=pt[:, :], func=AF.Copy), reads=[pkey], writes=latk(cc))
    cur = {}

    def kv_out(c):
        ob, okey, osem = ubr.next()
        cur["ob"], cur["sem"] = ob, osem
        return ob[:, 0:TT], [okey]

    def kv_after(c, okeys):
        S.dma("sp", ckvT[c * 128:(c + 1) * 128, ocols], cur["ob"][:, 0:TT], reads=okeys, sem=cur["sem"])
    rms(4, L["kvn"], kv_out, None, kv_after)
    rope_pair(wdkv[4], wdkv[5], KC, lambda k: hb[:, k, 2:XW], lambda k: [("hb", k, 1)], 1.0, krT[:, ocols])
    for c in range(KC):
        S.op("act", lambda e: e.activation(out=hb[:, c, 2:XW], in_=xt[:, c, 2:XW], func=AF.Identity,
                                           bias=vec2[:, L["sh3"] + c:L["sh3"] + c + 1], scale=vec2[:, SC3 + c:SC3 + c + 1]),
             reads=[("xt", c, 1), "vec"], writes=[("hb", c, 1)])
    for cc in range(8):
        pt, pkey, _ = psA.next()
        emit_proj(C, wdq[cc], wrot, KC, 128, lambda k: hb[:, k, 2:XW], lambda k: [("hb", k, 1)], pt, pkey, TT)
        S.op("act", lambda e: e.activation(out=lat(cc), in_=pt[:, :], func=AF.Copy), reads=[pkey], writes=latk(cc))
    rms(8, L["qn"], lambda c: (qdn[:, c, :], [("qdn", c)]), None)
    qscale = 192.0 ** -0.5
    for h in range(32):
        pt, pkey, _ = psA.next()
        emit_proj(C, wuq[h, 0], wrot, 8, 128, lambda k: qdn[:, k, :], lambda k: [("qdn", k)], pt, pkey, TT)
        ob, okey, osem = ubr.next()
        S.op("act", lambda e: e.activation(out=ob[:, 0:TT], in_=pt[:, :], func=AF.Copy, scale=qscale), reads=[pkey], writes=[okey])
        S.dma("sp", qnT[h][:, ocols], ob[:, 0:TT], reads=[okey], sem=osem)
        rope_pair(wuq[h, 1], wuq[h, 2], 8, lambda k: qdn[:, k, :], lambda k: [("qdn", k)], qscale, qrT[h][:, ocols])


_PROGS = {}
_DBG = {}


def _prog(name, fn, *a):
    key = (name,) + a
    if key not in _PROGS:
        _PROGS[key] = fn(*a)
    return _PROGS[key]


def _run(nc, in_maps):
    res = run_bass_kernel_spmd(nc, in_maps, core_ids=list(range(NCORE)))
    return res.results


def _chunks(w, ncols=128):
    Kd, N = w.shape
    return np.ascontiguousarray(w.reshape(Kd // 128, 128, N // ncols, ncols).transpose(2, 1, 0, 3))


def _pcol(v):
    return np.ascontiguousarray(v.reshape(-1, 128).T)


def _masks(kind):
    m = np.zeros((4, 128, QT), np.float32)
    k = np.arange(128)[:, None]
    q = np.arange(QT)[None, :]
    for o in range(4):
        kk = o * 128 + k
        if kind == "fox":
            bad = kk > q
        else:
            bad = (kk // 64) > (q // 64)
        m[o][np.broadcast_to(bad, (128, QT))] = NEG
    return m


def _rope_tables():
    inv = 10000.0 ** (-np.arange(0, 64, 2, dtype=np.float32) / np.float32(64))
    ang = np.arange(SEQ, dtype=np.float32)[:, None] * inv[None, :].astype(np.float32)
    cos = np.cos(ang).astype(np.float32).T
    sin = np.sin(ang).astype(np.float32).T
    Cc = np.concatenate([cos, cos], 0)
    Ss = np.concatenate([-sin, sin], 0)
    return np.ascontiguousarray(Cc), np.ascontiguousarray(Ss)


def _pad_cols(w, n=128):
    out = np.zeros(w.shape[:-1] + (n,), np.float32)
    out[..., :w.shape[-1]] = w
    return out


def _post_inputs(L, xfull, afull, ada_l, ln_g, ln_b, w_o, w_up, conv_w, conv_b, w_down, extras=None):
    lay = post_vec_layout(extras is not None)
    wo = _chunks(w_o)
    wup = _chunks(w_up)
    wdc = _chunks(w_down)
    wd = np.zeros((4, KC, 128, FQ, 128), np.float32)
    for qi in range(4):
        nq = QB[qi + 1] - QB[qi]
        wd[qi, :, :, 0:nq, :] = wdc[:, :, QB[qi]:QB[qi + 1], :]
    maps = []
    for core in range(NCORE):
        b, q = divmod(core, NCORE // B)
        t0 = q * TPC
        vec = np.zeros((128, lay["n"]), np.float32)

        def put(name, arr):
            vec[:, lay[name]:lay[name] + arr.shape[1]] = arr
        a0, a1 = ada_l[0][b], ada_l[1][b]
        put("gate1", _pcol(a0[2 * D:3 * D]))
        put("lng0", _pcol(ln_g[0])); put("lnb0", _pcol(ln_b[0]))
        put("sc2", _pcol(a1[0 * D + D:2 * D])); put("sh2", _pcol(a1[0:D])); put("gate2", _pcol(a1[2 * D:3 * D]))
        put("lng1", _pcol(ln_g[1])); put("lnb1", _pcol(ln_b[1]))
        for j in range(3):
            put("cw%d" % j, _pcol(conv_w[j]))
        put("cb", _pcol(conv_b))
        vec[:, lay["hflag"]] = 0.0 if q == 0 else 1.0
        xT = np.zeros((D, TPC + 2), np.float32)
        aT = np.zeros((D, TPC + 2), np.float32)
        lo = max(t0 - 2, 0)
        xT[:, 2 - (t0 - lo):] = xfull[b, lo:t0 + TPC].T
        aT[:, 2 - (t0 - lo):] = afull[b, lo:t0 + TPC].T
        m = dict(xT=xT, aT=aT, wo=wo, wup=wup, wdn=wd)
        if extras is not None:
            put("sc3", _pcol(extras["ada_next"][b][D:2 * D])); put("sh3", _pcol(extras["ada_next"][b][0:D]))
            put("kvn", _pcol(extras["kv_norm"])); put("qn", _pcol(extras["q_norm"]))
            m.update(ropeC=np.ascontiguousarray(extras["ropeC"][:, t0:t0 + TPC]),
                     ropeS=np.ascontiguousarray(extras["ropeS"][:, t0:t0 + TPC]),
                     wdkv=extras["wdkv"], wdq=extras["wdq"], wuq=extras["wuq"])
        m["vec"] = vec
        maps.append(m)
    return maps


def kernel(x, c, ada_w, ada_b, ln_g, ln_b, fox_w_qkv, fox_w_f, fox_b_f, fox_w_o,
           mla_w_dq, mla_q_norm, mla_w_uq, mla_w_o, mla_w_dkv, mla_kv_norm, mla_w_ukv,
           ffn_w_up, ffn_conv_w, ffn_conv_b, ffn_w_down):
    f = lambda a: np.asarray(a, dtype=np.float32)
    x, c, ada_w, ada_b, ln_g, ln_b = f(x), f(c), f(ada_w), f(ada_b), f(ln_g), f(ln_b)
    HPB = NCORE // B
    cT = np.ascontiguousarray(c.T.reshape(KC, 128, B).transpose(1, 0, 2))
    aw = ada_w.reshape(4, D, 3 * D)
    ab = ada_b.reshape(4, 3 * D)
    maps = []
    for i in range(NCORE):
        cols = slice(i * ADA_N, (i + 1) * ADA_N)
        bias = np.ascontiguousarray(np.broadcast_to(ab[:, cols].reshape(1, 4 * ADA_N), (B, 4 * ADA_N)))
        maps.append(dict(cT=cT, w=np.ascontiguousarray(aw[:, :, cols]), bias=bias))
    res = _run(_prog("ada", build_ada), maps)
    ada = np.concatenate([r["o"].reshape(B, 4, ADA_N) for r in res], axis=2)
    ada = ada.transpose(1, 0, 2).reshape(2, 2, B, 3 * D)
    _DBG['ada'] = ada

    wq = _chunks(f(fox_w_qkv)[0])
    wf = np.ascontiguousarray(f(fox_w_f)[0].reshape(KC, 128, 32).transpose(1, 0, 2))
    maps = []
    for core in range(NCORE):
        b, q = divmod(core, HPB)
        vec = np.zeros((128, PRE_VEC["n"]), np.float32)
        a0 = ada[0, 0, b]
        vec[:, 0:32] = _pcol(a0[D:2 * D])
        vec[:, 32:64] = _pcol(a0[0:D])
        vec[0:32, 64] = -f(fox_b_f)[0]
        maps.append(dict(xT=np.ascontiguousarray(x[b, q * TPC:(q + 1) * TPC].T), vec=vec, wq=wq, wf=wf))
    res = _run(_prog("pre0", build_pre0), maps)
    qkvT = np.stack([np.concatenate([res[b * HPB + q]["qkvT"] for q in range(HPB)], axis=2) for b in range(B)])
    logfT = np.stack([np.concatenate([res[b * HPB + q]["logfT"] for q in range(HPB)], axis=1) for b in range(B)])

    _DBG['qkvT'] = qkvT; _DBG['logfT'] = logfT
    mk = _masks("fox")
    maps = []
    for core in range(NCORE):
        b, hg = divmod(core, HPB)
        hsl = slice(hg * NH, (hg + 1) * NH)
        vt = np.ascontiguousarray(qkvT[b, 64 + hg * NH:64 + (hg + 1) * NH].transpose(0, 2, 1))
        maps.append(dict(qT=np.ascontiguousarray(qkvT[b, hsl]), kT=np.ascontiguousarray(qkvT[b, 32 + hg * NH:32 + (hg + 1) * NH]),
                         vtok=vt, logf=np.ascontiguousarray(logfT[b, hsl]), mask=mk))
    res = _run(_prog("att", build_att, "fox"), maps)
    att = np.stack([np.concatenate([res[b * HPB + hg]["oT"] for hg in range(HPB)], axis=0) for b in range(B)])
    att_tok = att.reshape(B, D, SEQ).transpose(0, 2, 1)

    _DBG['att0'] = att_tok
    ropeC, ropeS = _rope_tables()
    wdkv_full = f(mla_w_dkv)
    perm = np.concatenate([np.arange(32, 64), np.arange(0, 32)])
    kr_w = wdkv_full[:, 512:576]
    wdkv = np.concatenate([_chunks(wdkv_full[:, :512]), _chunks(_pad_cols(kr_w)), _chunks(_pad_cols(kr_w[:, perm]))], axis=0)
    wdq = _chunks(f(mla_w_dq)[0])
    wuq_full = f(mla_w_uq)[0].reshape(1024, 32, 192)
    wuq = np.zeros((32, 3, 128, 8, 128), np.float32)
    for h in range(32):
        wh = wuq_full[:, h, :]
        wuq[h, 0] = _chunks(wh[:, :128])[0]
        wuq[h, 1] = _chunks(_pad_cols(wh[:, 128:192]))[0]
        wuq[h, 2] = _chunks(_pad_cols(wh[:, 128:192][:, perm]))[0]
    extras = dict(ada_next=ada[1, 0], kv_norm=f(mla_kv_norm), q_norm=f(mla_q_norm)[0], ropeC=ropeC, ropeS=ropeS,
                  wdkv=wdkv, wdq=wdq, wuq=wuq)
    maps = _post_inputs(0, x, att_tok, ada[0][:, :, :], ln_g[0], ln_b[0], f(fox_w_o)[0], f(ffn_w_up)[0], f(ffn_conv_w)[0],
                        f(ffn_conv_b)[0], f(ffn_w_down)[0], extras)
    res = _run(_prog("post", build_post, True), maps)
    x1 = np.stack([np.concatenate([res[b * HPB + q]["xo"] for q in range(HPB)], axis=1).T for b in range(B)])
    cat = lambda name, ax: [np.concatenate([res[b * HPB + q][name] for q in range(HPB)], axis=ax) for b in range(B)]
    ckvT, krT, qnT, qrT = cat("ckvT", 1), cat("krT", 1), cat("qnT", 2), cat("qrT", 2)

    _DBG.update(x1=x1, ckvT=ckvT, krT=krT, qnT=qnT, qrT=qrT)
    mk = _masks("mla")
    wukv = f(mla_w_ukv).reshape(512, 32, 256)
    maps = []
    for core in range(NCORE):
        b, hg = divmod(core, HPB)
        hsl = slice(hg * NH, (hg + 1) * NH)
        wk = np.stack([_chunks(wukv[:, h, :128])[0] for h in range(hg * NH, (hg + 1) * NH)])
        wv = np.stack([_chunks(wukv[:, h, 128:])[0] for h in range(hg * NH, (hg + 1) * NH)])
        maps.append(dict(qT=np.ascontiguousarray(qnT[b][hsl]), qrT=np.ascontiguousarray(qrT[b][hsl]), ckvT=ckvT[b], krT=krT[b],
                         wk=wk, wv=wv, mask=mk))
    res = _run(_prog("att", build_att, "mla"), maps)
    att = np.stack([np.concatenate([res[b * HPB + hg]["oT"] for hg in range(HPB)], axis=0) for b in range(B)])
    att_tok = att.reshape(B, D, SEQ).transpose(0, 2, 1)

    _DBG['att1'] = att_tok
    maps = _post_inputs(1, x1, att_tok, ada[1][:, :, :], ln_g[1], ln_b[1], f(mla_w_o)[0], f(ffn_w_up)[1], f(ffn_conv_w)[1],
                        f(ffn_conv_b)[1], f(ffn_w_down)[1], None)
    res = _run(_prog("post", build_post, False), maps)
    out = np.stack([np.concatenate([res[b * HPB + q]["xo"] for q in range(HPB)], axis=1).T for b in range(B)])
    return np.ascontiguousarray(out.astype(np.float32))
```
